# Optimizing a Trainium2 kernel written in Bass

```python
import jax, jax.numpy as jnp
from jax import lax
import numpy as np

D_MODEL = 2048
BATCH = 2
SEQ = 8192
DEPTH = 4
DEC_BATCH = 2
DEC_SEQ = 16384
PAST_LEN = 128

MIX_DIM = D_MODEL
DN_HEADS = 8
DN_HEAD_DIM = 128
DN_DIM = DN_HEADS * DN_HEAD_DIM
SC_GROUPS = 8
SC_DIM = MIX_DIM - DN_DIM
SC_GROUP_DIM = SC_DIM // SC_GROUPS
CONV_WIDTH = 3
CHUNK = 64
D_FF = 5632
N_GATE_COLS = 4 * DN_HEADS
IN_COLS = 3 * DN_DIM + DN_DIM + N_GATE_COLS + 3 * SC_DIM
NORM_EPS = 1e-6
L2_EPS = 1e-6

kernel_name = 'hybrid_deltanet_shortconv_encoder'


def rms_norm(x, w):
    xf = x.astype(jnp.float32)
    y = xf * lax.rsqrt(jnp.mean(xf * xf, axis=-1, keepdims=True) + NORM_EPS)
    return (y * w.astype(jnp.float32)).astype(x.dtype)


def l2_norm(x):
    return x * lax.rsqrt(jnp.sum(x * x, axis=-1, keepdims=True) + L2_EPS)


def dwconv_centred(x, w):
    K = w.shape[0]
    p = K // 2
    T = x.shape[1]
    xp = jnp.pad(x, ((0, 0), (p, p), (0, 0)))
    return sum(xp[:, i:i + T] * w[i] for i in range(K))


def gated_delta_chunked(q, k, v, g, beta):
    N, T, H, Dk = q.shape
    Dv = v.shape[-1]
    NC = T // CHUNK

    def to_chunks(t):
        return t.reshape(N, NC, CHUNK, H, -1).transpose(1, 0, 3, 2, 4)

    q = to_chunks(q) * (Dk ** -0.5)
    k = to_chunks(k)
    v = to_chunks(v)
    g = g.reshape(N, NC, CHUNK, H).transpose(1, 0, 3, 2)
    beta = beta.reshape(N, NC, CHUNK, H).transpose(1, 0, 3, 2)
    g = jnp.cumsum(g, axis=-1)

    tri = jnp.tril(jnp.ones((CHUNK, CHUNK), dtype=bool))
    strict = jnp.tril(jnp.ones((CHUNK, CHUNK), dtype=bool), -1)
    eye = jnp.eye(CHUNK, dtype=q.dtype)
    diff = g[..., :, None] - g[..., None, :]
    decay = jnp.where(tri, jnp.exp(jnp.where(tri, diff, 0.0)), 0.0)

    kb = k * beta[..., None]
    vb = v * beta[..., None]
    a_kk = jnp.where(strict, jnp.einsum('znhcd,znhed->znhce', kb, k) * decay, 0.0)
    lower = a_kk + eye
    u = lax.linalg.triangular_solve(lower, vb, left_side=True, lower=True, unit_diagonal=True)
    w = lax.linalg.triangular_solve(lower, kb * jnp.exp(g)[..., None], left_side=True, lower=True, unit_diagonal=True)

    a_qk = jnp.einsum('znhcd,znhed->znhce', q, k) * decay
    qg = q * jnp.exp(g)[..., None]
    kdec = k * jnp.exp(g[..., -1:] - g)[..., None]
    glast = jnp.exp(g[..., -1])

    def step(S, xs):
        qg_i, kdec_i, u_i, w_i, a_i, gl_i = xs
        v_new = u_i - jnp.einsum('nhcd,nhde->nhce', w_i, S)
        o_i = jnp.einsum('nhcd,nhde->nhce', qg_i, S) + jnp.einsum('nhce,nhed->nhcd', a_i, v_new)
        S = S * gl_i[..., None, None] + jnp.einsum('nhcd,nhce->nhde', kdec_i, v_new)
        return S, o_i

    S0 = jnp.zeros((N, H, Dk, Dv), dtype=q.dtype)
    _, o = lax.scan(step, S0, (qg, kdec, u, w, a_qk, glast))
    return o.transpose(1, 0, 3, 2, 4).reshape(N, T, H, Dv)


def token_mixer(h, w_in, conv_qkv, a_log, dt_bias, dn_norm, conv_sc, sc_norm, w_out):
    B, T, _ = h.shape
    p = h @ w_in
    qkv, z, gl, sc = jnp.split(p, [3 * DN_DIM, 4 * DN_DIM, 4 * DN_DIM + N_GATE_COLS], axis=-1)

    qkv = jax.nn.silu(dwconv_centred(qkv, conv_qkv)).astype(jnp.float32)
    q, k, v = jnp.split(qkv, 3, axis=-1)
    q = l2_norm(q.reshape(B, T, DN_HEADS, DN_HEAD_DIM))
    k = l2_norm(k.reshape(B, T, DN_HEADS, DN_HEAD_DIM))
    v = v.reshape(B, T, DN_HEADS, DN_HEAD_DIM)
    gl = gl.astype(jnp.float32).reshape(B, T, 4, DN_HEADS)
    a_gate = gl[:, :, 0:2]
    b_gate = gl[:, :, 2:4]
    g = -jnp.exp(a_log.astype(jnp.float32)) * jax.nn.softplus(a_gate + dt_bias.astype(jnp.float32))
    beta = jax.nn.sigmoid(b_gate)
    flip = lambda t: jnp.flip(t, axis=1)
    q2 = jnp.concatenate([q, flip(q)], axis=0)
    k2 = jnp.concatenate([k, flip(k)], axis=0)
    v2 = jnp.concatenate([v, flip(v)], axis=0)
    g2 = jnp.concatenate([g[:, :, 0], flip(g[:, :, 1])], axis=0)
    beta2 = jnp.concatenate([beta[:, :, 0], flip(beta[:, :, 1])], axis=0)
    o2 = gated_delta_chunked(q2, k2, v2, g2, beta2)
    o = o2[:B] + flip(o2[B:])
    o = rms_norm(o, dn_norm) * jax.nn.silu(z.astype(jnp.float32).reshape(B, T, DN_HEADS, DN_HEAD_DIM))
    o_dn = o.reshape(B, T, DN_DIM).astype(h.dtype)

    b_g, c_g, x_sc = jnp.split(sc, 3, axis=-1)
    y = b_g * dwconv_centred(c_g * x_sc, conv_sc)
    y = rms_norm(y.reshape(B, T, SC_GROUPS, SC_GROUP_DIM), jnp.ones((SC_GROUP_DIM,), dtype=y.dtype))
    y_sc = y.reshape(B, T, SC_DIM) * sc_norm

    return jnp.concatenate([o_dn, y_sc], axis=-1) @ w_out


def conv_glu_ffn(h, w_up, conv_ffn, w_down):
    a, b = jnp.split(h @ w_up, 2, axis=-1)
    a = dwconv_centred(a, conv_ffn)
    return (jax.nn.silu(a) * b) @ w_down


def encoder_layer(x, norm_mix_pre, w_in, conv_qkv, a_log, dt_bias, dn_norm, conv_sc, sc_norm, w_out,
                  norm_mix_post, norm_ffn_pre, w_up, conv_ffn, w_down, norm_ffn_post):
    h = rms_norm(x, norm_mix_pre)
    x = x + rms_norm(token_mixer(h, w_in, conv_qkv, a_log, dt_bias, dn_norm, conv_sc, sc_norm, w_out), norm_mix_post)
    h = rms_norm(x, norm_ffn_pre)
    x = x + rms_norm(conv_glu_ffn(h, w_up, conv_ffn, w_down), norm_ffn_post)
    return x


def setup_inputs(seed: int = 0) -> dict:
    key = jax.random.key(seed)
    ks = jax.random.split(key, 20)
    f32 = jnp.float32
    nrm = lambda k, shape, s: jax.random.normal(k, shape, f32) * s
    gain = lambda k, shape: 1.0 + 0.05 * jax.random.normal(k, shape, f32)
    L = DEPTH
    dt = jnp.exp(jax.random.uniform(ks[5], (L, 2, DN_HEADS), f32, np.log(1e-3), np.log(1e-1)))
    return {
        'x_prompt': jax.random.normal(ks[0], (BATCH, SEQ, D_MODEL), f32),
        'x_sample': jax.random.normal(ks[1], (DEC_BATCH, DEC_SEQ, D_MODEL), f32),
        'norm_mix_pre': gain(ks[2], (L, D_MODEL)),
        'w_in': nrm(ks[3], (L, D_MODEL, IN_COLS), D_MODEL ** -0.5),
        'conv_qkv': nrm(ks[4], (L, CONV_WIDTH, 3 * DN_DIM), CONV_WIDTH ** -0.5),
        'a_log': jnp.log(jax.random.uniform(ks[6], (L, 2, DN_HEADS), f32, 1.0, 16.0)),
        'dt_bias': dt + jnp.log(-jnp.expm1(-dt)),
        'dn_norm': gain(ks[7], (L, DN_HEAD_DIM)),
        'conv_sc': nrm(ks[8], (L, CONV_WIDTH, SC_DIM), CONV_WIDTH ** -0.5),
        'sc_norm': gain(ks[9], (L, SC_DIM)),
        'w_out': nrm(ks[10], (L, MIX_DIM, D_MODEL), MIX_DIM ** -0.5),
        'norm_mix_post': gain(ks[11], (L, D_MODEL)),
        'norm_ffn_pre': gain(ks[12], (L, D_MODEL)),
        'w_up': nrm(ks[13], (L, D_MODEL, 2 * D_FF), D_MODEL ** -0.5),
        'conv_ffn': nrm(ks[14], (L, CONV_WIDTH, D_FF), CONV_WIDTH ** -0.5),
        'w_down': nrm(ks[15], (L, D_FF, D_MODEL), D_FF ** -0.5),
        'norm_ffn_post': gain(ks[16], (L, D_MODEL)),
    }


def reference(x_prompt, x_sample, norm_mix_pre, w_in, conv_qkv, a_log, dt_bias, dn_norm, conv_sc, sc_norm,
              w_out, norm_mix_post, norm_ffn_pre, w_up, conv_ffn, w_down, norm_ffn_post):
    y_prompt = x_prompt
    y_sample = x_sample
    for l in range(DEPTH):
        layer_params = (norm_mix_pre[l], w_in[l], conv_qkv[l], a_log[l], dt_bias[l], dn_norm[l], conv_sc[l],
                        sc_norm[l], w_out[l], norm_mix_post[l], norm_ffn_pre[l], w_up[l], conv_ffn[l], w_down[l],
                        norm_ffn_post[l])
        y_prompt = encoder_layer(y_prompt, *layer_params)
        y_sample = encoder_layer(y_sample, *layer_params)
    return (y_prompt, y_sample)
```

```python
import numpy as np
import ml_dtypes
import concourse.bass as bass
import concourse.mybir as mybir
from concourse.bass_utils import run_bass_kernel_spmd

F32 = mybir.dt.float32
BF16 = mybir.dt.bfloat16
AF = mybir.ActivationFunctionType
ALU = mybir.AluOpType
AX = mybir.AxisListType

D_MODEL = 2048
DEPTH = 4
DN_HEADS = 8
HD = 128
DN_DIM = 1024
SC_DIM = 1024
D_FF = 5632
NORM_EPS = 1e-6
L2_EPS = 1e-6
CH = 64


class Buf:
    __slots__ = ("name", "writers", "readers", "state", "dsem", "dcount")

    def __init__(self, name):
        self.name = name
        self.writers = {}
        self.readers = {}
        self.state = 0
        self.dsem = None
        self.dcount = 0


class Eng:
    def __init__(self, nc, h, name, pe=False):
        self.h = h
        self.name = name
        self.sem = nc.alloc_semaphore("c_" + name)
        self.n = 0
        self.waited = {}
        self.pe = pe


class Trk:
    def __init__(self, nc):
        self.nc = nc
        self.PE = Eng(nc, nc.tensor, "pe", pe=True)
        self.ACT = Eng(nc, nc.scalar, "act")
        self.DVE = Eng(nc, nc.vector, "dve")
        self.POOL = Eng(nc, nc.gpsimd, "pool")
        self.SP = Eng(nc, nc.sync, "sp")
        self.engs = [self.PE, self.ACT, self.DVE, self.POOL, self.SP]
        self.dma_sems = []
        self.nbuf = 0

    def buf(self, name=None):
        self.nbuf += 1
        return Buf(name or f"b{self.nbuf}")

    def _wait(self, e, deps):
        for sem, val in deps.items():
            if e.waited.get(sem, 0) < val:
                e.h.wait_ge(sem, val)
                e.waited[sem] = val

    def _collect(self, e, R, W):
        deps = {}

        def add(d):
            for s, v in d.items():
                if e.pe and s is e.sem:
                    continue
                if deps.get(s, 0) < v:
                    deps[s] = v
        for b in R:
            add(b.writers)
        for b in W:
            if b.state == 1:
                add(b.readers)
            else:
                add(b.writers)
                add(b.readers)
        return deps

    def _record(self, R, W, sem, val):
        for b in R:
            if b.state == 0:
                b.state = 1
            if b.readers.get(sem, 0) < val:
                b.readers[sem] = val
        for b in W:
            if b.state == 1:
                b.state = 0
                b.readers = {}
                b.writers = {}
            if b.writers.get(sem, 0) < val:
                b.writers[sem] = val

    def op(self, e, fn, R=(), W=()):
        self._wait(e, self._collect(e, R, W))
        ins = fn(e.h)
        e.n += 1
        ins.then_inc(e.sem, 1)
        self._record(R, W, e.sem, e.n)
        return ins

    def dma(self, e, out, in_, R=(), W=(), sb=None, **kw):
        self._wait(e, self._collect(e, R, W))
        if sb.dsem is None:
            sb.dsem = self.nc.alloc_semaphore("d_" + sb.name)
            self.dma_sems.append(sb)
        ins = e.h.dma_start(out=out, in_=in_, **kw)
        sb.dcount += 16
        ins.then_inc(sb.dsem, 16)
        self._record(R, W, sb.dsem, sb.dcount)
        return ins

    def barrier(self):
        deps = {}
        for e in self.engs:
            if e.n:
                deps[e.sem] = e.n
        for b in self.dma_sems:
            if b.dcount:
                deps[b.dsem] = b.dcount
        for e in self.engs:
            d = {s: v for s, v in deps.items() if not (s is e.sem)}
            self._wait(e, d)

    def free_dma_sems(self, bufs):
        for b in bufs:
            if b.dsem is not None:
                self.dma_sems.remove(b)


class Tl:
    def __init__(self, t, b):
        self.t = t
        self.b = b

    def __getitem__(self, k):
        return self.t[k]


from contextlib import ExitStack

NCOLS_L = 16 * 4 + 8 + 72 + 24 + 132
C_MIXPRE, C_MIXPOST, C_FFNPRE, C_FFNPOST, C_SCN, C_CQKV, C_CSC, C_CFFN = 0, 16, 32, 48, 64, 72, 144, 168

M_IDENT, M_ONES = 0, 1
M_U = (2, 3)
M_LC = (4, 5)
M_INCL = (6, 7)
M_NSTRICT = (8, 9)
M_BU = (10, 11)
M_SEL0, M_SEL1 = 12, 13
NMASK = 14


def host_consts():
    m = np.zeros((NMASK, 128, 128), np.float32)
    idx = np.arange(128)
    a = idx[:, None]
    b = idx[None, :]
    same = (a // CH) == (b // CH)
    m[M_IDENT] = np.eye(128)
    m[M_ONES] = 1.0
    m[M_U[0]] = (a <= b) & same
    m[M_U[1]] = (a >= b) & same
    m[M_LC[0]] = (a > b) & same
    m[M_LC[1]] = (a < b) & same
    m[M_INCL[0]] = (b <= a) & same
    m[M_INCL[1]] = (b >= a) & same
    m[M_NSTRICT[0]] = -1.0 * ((b < a) & same)
    m[M_NSTRICT[1]] = -1.0 * ((b > a) & same)
    m[M_BU[0]] = (a > b) & same
    m[M_BU[1]] = (a < b) & same
    m[M_SEL0] = (a < CH) * np.ones((128, 128))
    m[M_SEL1] = (a >= CH) * np.ones((128, 128))
    return np.ascontiguousarray(m.transpose(1, 0, 2))


class Ctx:
    pass


def win_tiles():
    t = [(n * 128, 128) for n in range(32)]
    t += [(4128 + n * 128, 128) for n in range(24)]
    t += [(4096, 32)]
    return t


_uid = [0]


class Phase:
    def __init__(self, G, name):
        self.G = G
        _uid[0] += 1
        self.name = f"{name}{_uid[0]}"

    def __enter__(self):
        self.es = ExitStack()
        self.G.dpool_i = 0
        return self

    def sb(self, name, shape, dt, dma=False):
        G = self.G
        t = self.es.enter_context(G.nc.sbuf_tensor(f"{self.name}_{name}", list(shape), dt))
        return Tl(t, G.take_dbuf() if dma else G.T.buf(name))

    def __exit__(self, *a):
        self.G.T.barrier()
        self.es.close()
        return False


def build_base(NTOK, depth=DEPTH, dbg=()):
    nc = bass.Bass("TRN2", target_bir_lowering=False)
    T = Trk(nc)
    G = Ctx()
    G.nc, G.T, G.NTOK, G.depth = nc, T, NTOK, depth
    NT128 = NTOK // 128

    def din(name, shape, dt=F32):
        return nc.dram_tensor(name, list(shape), dt, kind="ExternalInput").ap()

    def dscr(name, shape, dt):
        kind = "ExternalOutput" if name in dbg else "Internal"
        return nc.dram_tensor(name, list(shape), dt, kind=kind).ap()

    L = depth
    G.xT = din("xT", [D_MODEL, NTOK])
    G.w_in = din("w_in", [L, D_MODEL, 7200])
    G.w_out = din("w_out", [L, D_MODEL, D_MODEL])
    G.w_up = din("w_up", [L, D_MODEL, 2 * D_FF])
    G.w_down = din("w_down", [L, D_FF, D_MODEL])
    G.cols_d = din("cols", [128, L * NCOLS_L])
    G.gatep_d = din("gatep", [16, L * 2])
    G.dnw_d = din("dnw", [L, 128])
    G.masks_d = din("masks", [128, NMASK, 128])
    G.flags_d = din("flags", [128, 4])
    G.yT = nc.dram_tensor("yT", [D_MODEL, NTOK], F32, kind="ExternalOutput").ap()

    G.wb_in = [dscr(f"wb_in{l}", [57, 128, 16 * 128], BF16) for l in range(L)]
    G.wb_out = [dscr(f"wb_out{l}", [16, 128, 16 * 128], BF16) for l in range(L)]
    G.wb_up = [dscr(f"wb_up{l}", [88, 128, 16 * 128], BF16) for l in range(L)]
    G.wb_down = [dscr(f"wb_down{l}", [16, 128, 44 * 128], BF16) for l in range(L)]
    G.pT = dscr("pT", [7168, NTOK + 2], BF16)
    G.graw = dscr("graw", [32, NTOK], F32)
    G.gbT = dscr("gbT", [32, NTOK], F32)
    G.qkvT = dscr("qkvT", [3072, NTOK], BF16)
    G.szT = dscr("szT", [1024, NTOK], BF16)
    G.mixT = dscr("mixT", [2048, NTOK], BF16)
    G.xmT = dscr("xmT", [D_MODEL, NTOK], F32)
    G.xsT = [dscr(f"xsT{i}", [D_MODEL, NTOK], F32) for i in range(2)]
    G.upA = dscr("upA", [D_FF, NTOK + 2], BF16)
    G.upB = dscr("upB", [D_FF, NTOK], BF16)
    G.gfT = dscr("gfT", [D_FF, NTOK], BF16)
    G.dn_wT = [dscr(f"dn_wT{d}", [NT128, 128, 1024], BF16) for d in range(2)]
    G.dn_u = [dscr(f"dn_u{d}", [NT128, 128, 1024], F32) for d in range(2)]
    G.dn_aT = [dscr(f"dn_aT{d}", [NT128, 128, 1024], BF16) for d in range(2)]
    G.dn_kd = [dscr(f"dn_kd{d}", [NT128, 128, 1024], BF16) for d in range(2)]
    G.dn_qg = [dscr(f"dn_qg{d}", [NT128, 128, 1024], BF16) for d in range(2)]
    G.dn_sc = [dscr(f"dn_sc{d}", [NT128, 128, 32], F32) for d in range(2)]
    G.dn_o = [dscr(f"dn_o{d}", [NTOK, 1024], F32) for d in range(2)]
    G.db = {}

    def dbuf(name):
        if name not in G.db:
            G.db[name] = T.buf("dram_" + name)
        return G.db[name]
    G.dbuf = dbuf

    G.cols = Tl(nc.alloc_sbuf_tensor("sb_cols", [128, L * NCOLS_L], F32), T.buf("cols"))
    G.masks = Tl(nc.alloc_sbuf_tensor("sb_masks", [128, NMASK, 128], F32), T.buf("masks"))
    G.flags = Tl(nc.alloc_sbuf_tensor("sb_flags", [128, 4], F32), T.buf("flags"))
    G.identb = Tl(nc.alloc_sbuf_tensor("identb", [128, 128], BF16), T.buf("identb"))
    G.gatep = Tl(nc.alloc_sbuf_tensor("sb_gatep", [16, L * 2], F32), T.buf("gatep"))
    T.dma(T.SP, G.cols[:], G.cols_d[:, :], W=[G.cols.b], sb=G.cols.b)
    T.dma(T.SP, G.masks[:], G.masks_d[:, :, :], W=[G.masks.b], sb=G.masks.b)
    T.dma(T.SP, G.flags[:], G.flags_d[:, :], W=[G.flags.b], sb=G.flags.b)
    T.dma(T.SP, G.gatep[:], G.gatep_d[:, :], W=[G.gatep.b], sb=G.gatep.b)
    T.op(T.DVE, lambda h: h.tensor_copy(out=G.identb[:], in_=G.masks[:, M_IDENT, :]),
         R=[G.masks.b], W=[G.identb.b])
    G.PS = [nc.alloc_psum_tensor(f"ps{i}", [128, 1024], F32) for i in range(4)]
    G.PB = [T.buf(f"psb{i}") for i in range(8)]

    G.dpool = []
    G.dpool_i = 0

    def take_dbuf():
        if G.dpool_i == len(G.dpool):
            G.dpool.append(T.buf(f"dp{len(G.dpool)}"))
        b = G.dpool[G.dpool_i]
        G.dpool_i += 1
        return b
    G.take_dbuf = take_dbuf
    G.col = lambda l, c, n=1: G.cols[:, l * NCOLS_L + c: l * NCOLS_L + c + n]
    G.ones = G.masks[:, M_ONES, :]
    return G


def phase_wcast(G, l):
    T = G.T
    with Phase(G, "wc") as ph:
        st32 = [ph.sb(f"s32_{i}", [128, 16 * 512], F32, dma=True) for i in range(2)]
        st16 = [ph.sb(f"s16_{i}", [128, 16 * 512], BF16, dma=True) for i in range(2)]
        cnt = [0]

        def do(W, K, tiles, Wb, wname):
            KT = K // 128
            gmax = 4 if KT == 16 else 1
            i = 0
            while i < len(tiles):
                c0, ncl = tiles[i]
                g = 1
                if ncl == 128:
                    while g < gmax and i + g < len(tiles) and tiles[i + g] == (c0 + g * 128, 128):
                        g += 1
                width = ncl if g == 1 else g * 128
                cw = width // g
                s32 = st32[cnt[0] % 2]
                s16 = st16[cnt[0] % 2]
                src = W[:, c0:c0 + width].rearrange("(kt p) c -> p kt c", p=128)
                dst32 = s32[:, 0:KT * width].rearrange("p (kt c) -> p kt c", kt=KT)
                T.dma(T.SP, dst32, src, W=[s32.b], sb=s32.b)
                in_v = s32[:, 0:KT * width].rearrange("p (kt g c) -> p g kt c", kt=KT, g=g)
                out_v = s16[:, 0:KT * width].rearrange("p (g kt c) -> p g kt c", g=g, kt=KT)
                which = cnt[0] % 2
                if which == 0:
                    T.op(T.DVE, lambda h: h.tensor_copy(out=out_v, in_=in_v), R=[s32.b], W=[s16.b])
                else:
                    T.op(T.ACT, lambda h: h.copy(out=out_v, in_=in_v), R=[s32.b], W=[s16.b])
                if ncl == 128:
                    dstd = Wb[i:i + g, :, :].rearrange("g p f -> p g f")
                    srcs = s16[:, 0:KT * width].rearrange("p (g f) -> p g f", g=g)
                else:
                    dstd = Wb[i, :, :].rearrange("p (kt c) -> p kt c", kt=KT)[:, :, 0:ncl]
                    srcs = s16[:, 0:KT * width].rearrange("p (kt c) -> p kt c", kt=KT)
                T.dma(T.SP, dstd, srcs, R=[s16.b], W=[G.dbuf(wname)], sb=s16.b)
                cnt[0] += 1
                i += g

        do(G.w_in[l], D_MODEL, win_tiles(), G.wb_in[l], f"wb_in{l}")
        do(G.w_out[l], D_MODEL, [(n * 128, 128) for n in range(16)], G.wb_out[l], f"wb_out{l}")
        do(G.w_up[l], D_MODEL, [(n * 128, 128) for n in range(88)], G.wb_up[l], f"wb_up{l}")
        do(G.w_down[l], D_FF, [(n * 128, 128) for n in range(16)], G.wb_down[l], f"wb_down{l}")


def phase_proj(G, l, kind):
    T, nc, NTOK = G.T, G.nc, G.NTOK
    PS, PB = G.PS, G.PB
    if kind == "in":
        src, srcn, K, normed, gcol, Wb, wname = G.x_cur, G.x_cur_name, D_MODEL, True, C_MIXPRE, G.wb_in[l], f"wb_in{l}"
        tiles = win_tiles()
        TB = 1024
    elif kind == "up":
        src, srcn, K, normed, gcol, Wb, wname = G.xmT, "xmT", D_MODEL, True, C_FFNPRE, G.wb_up[l], f"wb_up{l}"
        tiles = [(n * 128, 128) for n in range(88)]
        TB = 1024
    elif kind == "out":
        src, srcn, K, normed, Wb, wname = G.mixT, "mixT", D_MODEL, False, G.wb_out[l], f"wb_out{l}"
        tiles = [(n * 128, 128) for n in range(16)]
        TB = 1024
        xold, xoldn, dst, dstn, pcol = G.x_cur, G.x_cur_name, G.xmT, "xmT", C_MIXPOST
    else:
        src, srcn, K, normed, Wb, wname = G.gfT, "gfT", D_FF, False, G.wb_down[l], f"wb_down{l}"
        tiles = [(n * 128, 128) for n in range(16)]
        TB = 512
        xold, xoldn, dst, dstn, pcol = G.xmT, "xmT", G.x_next, G.x_next_name, C_FFNPOST
    TB = min(TB, NTOK)
    res = kind in ("out", "down")
    KT = K // 128
    NH = TB // 512
    NTL = len(tiles)
    NBLK = NTOK // TB
    eps_b = NORM_EPS

    with Phase(G, "pj" + kind) as ph:
        hT = ph.sb("hT", [128, KT, TB], BF16)
        GK = 4 if not normed else 1
        hb = [G.take_dbuf() if not normed else T.buf(f"hT{k}") for k in range((KT + GK - 1) // GK)]
        wr = [ph.sb(f"w{i}", [128, KT * 128], BF16, dma=True) for i in range(3)]
        rs = ph.sb("rs", [128, TB], F32)
        lnv = ph.sb("lnv", [128, TB], F32)
        sq = [ph.sb(f"sq{i}", [128, TB], F32) for i in range(2)]
        xin = [ph.sb(f"xin{i}", [128, TB], F32, dma=True) for i in range(3)]
        if res:
            ysb = ph.sb("ysb", [128, 16, TB], F32)
            yb = [T.buf(f"y{n}") for n in range(16)]
            xo = [ph.sb(f"xo{i}", [128, TB], F32, dma=True) for i in range(2)]
        else:
            ob = [ph.sb(f"ob{i}", [128, TB], BF16, dma=True) for i in range(3)]
            og = ph.sb("og", [32, TB], F32, dma=True)
        wcnt = 0
        xcnt = 0
        ocnt = 0
        sqc = 0
        for blk in range(NBLK):
            t0 = blk * TB
            stat_bufs = [PB[4], PB[5]][:NH]
            if normed:
                for kt in range(KT):
                    xi = xin[xcnt % 3]
                    xcnt += 1
                    T.dma(T.SP, xi[:], src[kt * 128:(kt + 1) * 128, t0:t0 + TB], R=[G.dbuf(srcn)], W=[xi.b], sb=xi.b)
                    s = sq[sqc % 2]
                    sqc += 1
                    T.op(T.ACT, lambda h: h.activation(out=s[:], in_=xi[:], func=AF.Square), R=[xi.b], W=[s.b])
                    for hh in range(NH):
                        T.op(T.PE, lambda h: h.matmul(PS[2][:, hh * 512:(hh + 1) * 512], lhsT=G.ones, rhs=s[:, hh * 512:(hh + 1) * 512],
                                                      start=(kt == 0), stop=(kt == KT - 1)),
                             R=[s.b, G.masks.b], W=[stat_bufs[hh]])
                    T.op(T.DVE, lambda h: h.tensor_scalar(out=hT[:, kt, :], in0=xi[:], scalar1=G.col(l, gcol + kt), scalar2=None, op0=ALU.mult),
                         R=[xi.b, G.cols.b], W=[hb[kt]])
                T.op(T.ACT, lambda h: h.activation(out=lnv[:], in_=PS[2][:, 0:TB], func=AF.Ln, scale=1.0 / K, bias=eps_b),
                     R=stat_bufs, W=[lnv.b])
                T.op(T.ACT, lambda h: h.activation(out=rs[:], in_=lnv[:], func=AF.Exp, scale=-0.5), R=[lnv.b], W=[rs.b])
            else:
                for gi in range(len(hb)):
                    k0, k1 = gi * GK, min(KT, (gi + 1) * GK)
                    T.dma(T.SP, hT[:, k0:k1, :], src[k0 * 128:k1 * 128, t0:t0 + TB].rearrange("(kt p) t -> p kt t", p=128),
                          R=[G.dbuf(srcn)], W=[hb[gi]], sb=hb[gi])
            for nt in range(NTL):
                c0, M = tiles[nt]
                w = wr[wcnt % 3]
                wcnt += 1
                T.dma(T.SP, w[:], Wb[nt, :, :], R=[G.dbuf(wname)], W=[w.b], sb=w.b)
                pa = PS[nt % 2]
                pbs = [PB[2 * (nt % 2)], PB[2 * (nt % 2) + 1]][:NH]
                for kt in range(KT):
                    for hh in range(NH):
                        T.op(T.PE, lambda h: h.matmul(pa[0:M, hh * 512:(hh + 1) * 512], lhsT=w[:, kt * 128:kt * 128 + M],
                                                      rhs=hT[:, kt, hh * 512:(hh + 1) * 512], start=(kt == 0), stop=(kt == KT - 1)),
                             R=[w.b, hb[kt // GK]], W=[pbs[hh]])
                if not res:
                    if M == 128:
                        o = ob[ocnt % 3]
                        ocnt += 1
                    else:
                        o = og
                    T.op(T.DVE, lambda h: h.tensor_tensor(out=o[0:M, :], in0=pa[0:M, 0:TB], in1=rs[0:M, :], op=ALU.mult),
                         R=pbs + [rs.b], W=[o.b])
                    if kind == "in":
                        if M == 128:
                            d_ap, dn = G.pT[nt * 128:(nt + 1) * 128, 1 + t0:1 + t0 + TB], "pT"
                        else:
                            d_ap, dn = G.graw[0:32, t0:t0 + TB], "graw"
                    else:
                        if nt < 44:
                            d_ap, dn = G.upA[nt * 128:(nt + 1) * 128, 1 + t0:1 + t0 + TB], "upA"
                        else:
                            d_ap, dn = G.upB[(nt - 44) * 128:(nt - 43) * 128, t0:t0 + TB], "upB"
                    T.dma(T.ACT, d_ap, o[0:M, :], R=[o.b], W=[G.dbuf(dn)], sb=o.b)
                else:
                    T.op(T.ACT, lambda h: h.copy(out=ysb[:, nt, :], in_=pa[:, 0:TB]), R=pbs, W=[yb[nt]])
                    s = sq[sqc % 2]
                    sqc += 1
                    T.op(T.ACT, lambda h: h.activation(out=s[:], in_=pa[:, 0:TB], func=AF.Square), R=pbs, W=[s.b])
                    for hh in range(NH):
                        T.op(T.PE, lambda h: h.matmul(PS[2][:, hh * 512:(hh + 1) * 512], lhsT=G.ones, rhs=s[:, hh * 512:(hh + 1) * 512],
                                                      start=(nt == 0), stop=(nt == NTL - 1)),
                             R=[s.b, G.masks.b], W=[stat_bufs[hh]])
            if res:
                T.op(T.ACT, lambda h: h.activation(out=lnv[:], in_=PS[2][:, 0:TB], func=AF.Ln, scale=1.0 / D_MODEL, bias=eps_b),
                     R=stat_bufs, W=[lnv.b])
                T.op(T.ACT, lambda h: h.activation(out=rs[:], in_=lnv[:], func=AF.Exp, scale=-0.5), R=[lnv.b], W=[rs.b])
                for nt in range(16):
                    xi = xin[xcnt % 3]
                    xcnt += 1
                    T.dma(T.SP, xi[:], xold[nt * 128:(nt + 1) * 128, t0:t0 + TB], R=[G.dbuf(xoldn)], W=[xi.b], sb=xi.b)
                    T.op(T.DVE, lambda h: h.tensor_tensor(out=ysb[:, nt, :], in0=ysb[:, nt, :], in1=rs[:], op=ALU.mult),
                         R=[yb[nt], rs.b], W=[yb[nt]])
                    o = xo[ocnt % 2]
                    ocnt += 1
                    T.op(T.DVE, lambda h: h.scalar_tensor_tensor(out=o[:], in0=ysb[:, nt, :], scalar=G.col(l, pcol + nt), in1=xi[:],
                                                                 op0=ALU.mult, op1=ALU.add),
                         R=[yb[nt], xi.b, G.cols.b], W=[o.b])
                    T.dma(T.ACT, dst[nt * 128:(nt + 1) * 128, t0:t0 + TB], o[:], R=[o.b], W=[G.dbuf(dstn)], sb=o.b)


def build_program(NTOK, depth=DEPTH, stop_after=None, dbg=()):
    G = build_base(NTOK, depth, dbg)
    T = G.T
    stages = []

    def stage(name, fn):
        if G.stopped:
            return
        fn()
        if stop_after == name:
            G.stopped = True
    G.stopped = False
    stage("init", lambda: phase_init(G))
    for l in range(depth):
        G.x_cur, G.x_cur_name = (G.xT, "xT") if l == 0 else (G.xsT[(l - 1) % 2], f"xsT{(l - 1) % 2}")
        G.x_next, G.x_next_name = (G.yT, "yT") if l == depth - 1 else (G.xsT[l % 2], f"xsT{l % 2}")
        stage(f"wcast{l}", lambda: phase_wcast(G, l))
        stage(f"pin{l}", lambda: phase_proj(G, l, "in"))
        if "phase_e1" in globals():
            stage(f"e1{l}", lambda: phase_e1(G, l))
            stage(f"dn1{l}", lambda: phase_dn1(G, l))
            stage(f"dn2{l}", lambda: phase_dn2(G, l))
            stage(f"dn3{l}", lambda: phase_dn3(G, l))
            stage(f"pout{l}", lambda: phase_proj(G, l, "out"))
            stage(f"pup{l}", lambda: phase_proj(G, l, "up"))
            stage(f"e2{l}", lambda: phase_e2(G, l))
            stage(f"pdown{l}", lambda: phase_proj(G, l, "down"))
    T.barrier()
    return G


def prep_small(inp, depth):
    L = depth
    cols = np.zeros((128, L * NCOLS_L), np.float32)
    for l in range(L):
        b = l * NCOLS_L
        cols[:, b + C_MIXPRE:b + C_MIXPRE + 16] = np.asarray(inp["norm_mix_pre"][l]).reshape(16, 128).T
        cols[:, b + C_MIXPOST:b + C_MIXPOST + 16] = np.asarray(inp["norm_mix_post"][l]).reshape(16, 128).T
        cols[:, b + C_FFNPRE:b + C_FFNPRE + 16] = np.asarray(inp["norm_ffn_pre"][l]).reshape(16, 128).T
        cols[:, b + C_FFNPOST:b + C_FFNPOST + 16] = np.asarray(inp["norm_ffn_post"][l]).reshape(16, 128).T
        cols[:, b + C_SCN:b + C_SCN + 8] = np.asarray(inp["sc_norm"][l]).reshape(8, 128).T
        cq = np.asarray(inp["conv_qkv"][l]).reshape(3, 24, 128)
        cols[:, b + C_CQKV:b + C_CQKV + 72] = cq.transpose(2, 1, 0).reshape(128, 72)
        cs = np.asarray(inp["conv_sc"][l]).reshape(3, 8, 128)
        cols[:, b + C_CSC:b + C_CSC + 24] = cs.transpose(2, 1, 0).reshape(128, 24)
        cf = np.asarray(inp["conv_ffn"][l]).reshape(3, 44, 128)
        cols[:, b + C_CFFN:b + C_CFFN + 132] = cf.transpose(2, 1, 0).reshape(128, 132)
    gatep = np.zeros((16, L * 2), np.float32)
    for l in range(L):
        gatep[:, 2 * l] = np.asarray(inp["a_log"][l]).reshape(16)
        gatep[:, 2 * l + 1] = np.asarray(inp["dt_bias"][l]).reshape(16)
    dnw = np.ascontiguousarray(np.asarray(inp["dn_norm"][:L], np.float32))
    return cols, gatep, dnw


def bank_ap(G, i):
    return G.PS[i // 2][:, (i % 2) * 512:(i % 2) * 512 + 512]


def halo_fix(G, tl, t0, TBE, W):
    T = G.T
    mid = G.NTOK // 2
    if t0 == mid:
        T.op(T.POOL, lambda h: h.tensor_scalar(out=tl[:, 0:1], in0=tl[:, 0:1], scalar1=G.flags[:, 0:1], scalar2=None, op0=ALU.mult),
             R=[tl.b, G.flags.b], W=[tl.b])
    if t0 + TBE == mid:
        T.op(T.POOL, lambda h: h.tensor_scalar(out=tl[:, W - 1:W], in0=tl[:, W - 1:W], scalar1=G.flags[:, 0:1], scalar2=None, op0=ALU.mult),
             R=[tl.b, G.flags.b], W=[tl.b])


def make_diags(G, ph_diag, l, cbase, ct):
    T = G.T
    for tap in range(3):
        T.op(T.POOL, lambda h: h.tensor_scalar(out=ph_diag[:, tap, :], in0=G.masks[:, M_IDENT, :], scalar1=G.col(l, cbase + ct * 3 + tap),
                                               scalar2=None, op0=ALU.mult),
             R=[G.masks.b, G.cols.b], W=[ph_diag.b])


def phase_e1(G, l):
    T, nc, NTOK = G.T, G.nc, G.NTOK
    PB = G.PB
    TBE = 512
    NBE = NTOK // TBE
    SEG = min(NTOK, 8192)
    NSEG = NTOK // SEG
    BPS = SEG // TBE
    with Phase(G, "e1") as ph:
        diag = ph.sb("diag", [128, 3, 128], BF16)
        pin = [ph.sb(f"pin{i}", [128, TBE + 2], BF16, dma=True) for i in range(3)]
        pin2 = [ph.sb(f"pinb{i}", [128, TBE + 2], BF16, dma=True) for i in range(3)]
        pin3 = [ph.sb(f"pinc{i}", [128, TBE], BF16, dma=True) for i in range(3)]
        cx = [ph.sb(f"cx{i}", [128, TBE + 2], BF16) for i in range(2)]
        S = ph.sb("S", [128, SEG], F32)
        SS = ph.sb("SS", [128, SEG], F32)
        Sb = [T.buf(f"S{i}") for i in range(BPS)]
        SSb = [T.buf(f"SS{i}") for i in range(BPS)]
        sq = [ph.sb(f"sq{i}", [128, TBE], F32) for i in range(2)]
        ob = [ph.sb(f"ob{i}", [128, TBE], BF16, dma=True) for i in range(3)]
        cnt = dict(p=0, b=0, o=0, s=0, c=0)

        def load_p(row0, t0, ring, halo=True):
            tl = ring[cnt["p"] % 3]
            cnt["p"] += 1
            if halo:
                T.dma(T.SP, tl[:], G.pT[row0:row0 + 128, t0:t0 + TBE + 2], R=[G.dbuf("pT")], W=[tl.b], sb=tl.b)
                halo_fix(G, tl, t0, TBE, TBE + 2)
            else:
                T.dma(T.SP, tl[:], G.pT[row0:row0 + 128, 1 + t0:1 + t0 + TBE], R=[G.dbuf("pT")], W=[tl.b], sb=tl.b)
            return tl

        def conv(src_tl):
            bi = cnt["b"] % 4
            cnt["b"] += 1
            for tap in range(3):
                T.op(T.PE, lambda h: h.matmul(bank_ap(G, bi), lhsT=diag[:, tap, :], rhs=src_tl[:, tap:tap + TBE], start=(tap == 0), stop=(tap == 2)),
                     R=[diag.b, src_tl.b], W=[PB[bi]])
            return bi

        def stats(src_ap, src_buf):
            s = sq[cnt["s"] % 2]
            cnt["s"] += 1
            T.op(T.ACT, lambda h: h.activation(out=s[:], in_=src_ap, func=AF.Square), R=[src_buf], W=[s.b])
            bi = 4 + (cnt["s"] % 2)
            T.op(T.PE, lambda h: h.matmul(bank_ap(G, bi), lhsT=G.ones, rhs=s[:], start=True, stop=True), R=[s.b, G.masks.b], W=[PB[bi]])
            return bi

        def store(o, dst_ap, dname):
            T.dma(T.ACT, dst_ap, o[:], R=[o.b], W=[G.dbuf(dname)], sb=o.b)

        def next_ob():
            o = ob[cnt["o"] % 3]
            cnt["o"] += 1
            return o

        for h_ in range(8):
            for blk in range(NBE):
                t0 = blk * TBE
                tl = load_p(3072 + h_ * 128, t0, pin3, halo=False)
                o = next_ob()
                T.op(T.ACT, lambda h: h.activation(out=o[:], in_=tl[:], func=AF.Silu), R=[tl.b], W=[o.b])
                store(o, G.szT[h_ * 128:(h_ + 1) * 128, t0:t0 + TBE], "szT")
        for h_ in range(8):
            make_diags(G, diag, l, C_CQKV, 16 + h_)
            for blk in range(NBE):
                t0 = blk * TBE
                tl = load_p(2048 + h_ * 128, t0, pin)
                bi = conv(tl)
                o = next_ob()
                T.op(T.ACT, lambda h: h.activation(out=o[:], in_=bank_ap(G, bi), func=AF.Silu), R=[PB[bi]], W=[o.b])
                store(o, G.qkvT[2048 + h_ * 128:2048 + (h_ + 1) * 128, t0:t0 + TBE], "qkvT")
        for which in range(2):
            for h_ in range(8):
                make_diags(G, diag, l, C_CQKV, which * 8 + h_)
                r0 = which * 1024 + h_ * 128
                for seg in range(NSEG):
                    for b_ in range(BPS):
                        t0 = seg * SEG + b_ * TBE
                        tl = load_p(r0, t0, pin)
                        bi = conv(tl)
                        sl = S[:, b_ * TBE:(b_ + 1) * TBE]
                        T.op(T.ACT, lambda h: h.activation(out=sl, in_=bank_ap(G, bi), func=AF.Silu), R=[PB[bi]], W=[Sb[b_]])
                        si = stats(sl, Sb[b_])
                        T.op(T.DVE, lambda h: h.tensor_scalar(out=SS[:, b_ * TBE:(b_ + 1) * TBE], in0=bank_ap(G, si), scalar1=L2_EPS, scalar2=None, op0=ALU.add),
                             R=[PB[si]], W=[SSb[b_]])
                    for b_ in range(BPS):
                        ssl = SS[:, b_ * TBE:(b_ + 1) * TBE]
                        T.op(T.ACT, lambda h: h.activation(out=ssl, in_=ssl, func=AF.Ln), R=[SSb[b_]], W=[SSb[b_]])
                        T.op(T.ACT, lambda h: h.activation(out=ssl, in_=ssl, func=AF.Exp, scale=-0.5), R=[SSb[b_]], W=[SSb[b_]])
                    for b_ in range(BPS):
                        t0 = seg * SEG + b_ * TBE
                        o = next_ob()
                        T.op(T.DVE, lambda h: h.scalar_tensor_tensor(out=o[:], in0=S[:, b_ * TBE:(b_ + 1) * TBE], scalar=(HD ** -0.5 if which == 0 else 1.0),
                                                                     in1=SS[:, b_ * TBE:(b_ + 1) * TBE], op0=ALU.mult, op1=ALU.mult),
                             R=[Sb[b_], SSb[b_]], W=[o.b])
                        store(o, G.qkvT[r0:r0 + 128, t0:t0 + TBE], "qkvT")
        for g_ in range(8):
            make_diags(G, diag, l, C_CSC, g_)
            for seg in range(NSEG):
                for b_ in range(BPS):
                    t0 = seg * SEG + b_ * TBE
                    bt = load_p(4096 + g_ * 128, t0, pin3, halo=False)
                    ct_ = load_p(5120 + g_ * 128, t0, pin)
                    xt_ = load_p(6144 + g_ * 128, t0, pin2)
                    c = cx[cnt["c"] % 2]
                    cnt["c"] += 1
                    T.op(T.POOL, lambda h: h.tensor_tensor(out=c[:], in0=ct_[:], in1=xt_[:], op=ALU.mult), R=[ct_.b, xt_.b], W=[c.b])
                    bi = conv(c)
                    sl = S[:, b_ * TBE:(b_ + 1) * TBE]
                    T.op(T.DVE, lambda h: h.tensor_tensor(out=sl, in0=bank_ap(G, bi), in1=bt[:], op=ALU.mult), R=[PB[bi], bt.b], W=[Sb[b_]])
                    si = stats(sl, Sb[b_])
                    T.op(T.DVE, lambda h: h.tensor_scalar(out=SS[:, b_ * TBE:(b_ + 1) * TBE], in0=bank_ap(G, si), scalar1=1.0 / 128, scalar2=NORM_EPS,
                                                          op0=ALU.mult, op1=ALU.add),
                         R=[PB[si]], W=[SSb[b_]])
                for b_ in range(BPS):
                    ssl = SS[:, b_ * TBE:(b_ + 1) * TBE]
                    T.op(T.ACT, lambda h: h.activation(out=ssl, in_=ssl, func=AF.Ln), R=[SSb[b_]], W=[SSb[b_]])
                    T.op(T.ACT, lambda h: h.activation(out=ssl, in_=ssl, func=AF.Exp, scale=-0.5), R=[SSb[b_]], W=[SSb[b_]])
                for b_ in range(BPS):
                    t0 = seg * SEG + b_ * TBE
                    o = next_ob()
                    T.op(T.DVE, lambda h: h.scalar_tensor_tensor(out=o[:], in0=S[:, b_ * TBE:(b_ + 1) * TBE], scalar=G.col(l, C_SCN + g_),
                                                                 in1=SS[:, b_ * TBE:(b_ + 1) * TBE], op0=ALU.mult, op1=ALU.mult),
                         R=[Sb[b_], SSb[b_], G.cols.b], W=[o.b])
                    store(o, G.mixT[1024 + g_ * 128:1024 + (g_ + 1) * 128, t0:t0 + TBE], "mixT")
        TG = min(NTOK, 2048)
        ga = [ph.sb(f"ga{i}", [16, TG], F32, dma=True) for i in range(2)]
        gb = [ph.sb(f"gb{i}", [16, TG], F32, dma=True) for i in range(2)]
        go = [ph.sb(f"go{i}", [16, TG], F32, dma=True) for i in range(2)]
        go2 = [ph.sb(f"goo{i}", [16, TG], F32, dma=True) for i in range(2)]
        nA = ph.sb("nA", [16, 1], F32)
        T.op(T.ACT, lambda h: h.activation(out=nA[:], in_=G.gatep[:, 2 * l:2 * l + 1], func=AF.Exp), R=[G.gatep.b], W=[nA.b])
        T.op(T.DVE, lambda h: h.tensor_scalar(out=nA[:], in0=nA[:], scalar1=-1.0, scalar2=None, op0=ALU.mult), R=[nA.b], W=[nA.b])
        for i in range(NTOK // TG):
            t0 = i * TG
            a_t, b_t, o1, o2 = ga[i % 2], gb[i % 2], go[i % 2], go2[i % 2]
            T.dma(T.SP, a_t[:], G.graw[0:16, t0:t0 + TG], R=[G.dbuf("graw")], W=[a_t.b], sb=a_t.b)
            T.dma(T.SP, b_t[:], G.graw[16:32, t0:t0 + TG], R=[G.dbuf("graw")], W=[b_t.b], sb=b_t.b)
            T.op(T.ACT, lambda h: h.activation(out=a_t[:], in_=a_t[:], func=AF.Exp, bias=G.gatep[:, 2 * l + 1:2 * l + 2]), R=[a_t.b, G.gatep.b], W=[a_t.b])
            T.op(T.ACT, lambda h: h.activation(out=a_t[:], in_=a_t[:], func=AF.Ln, bias=1.0), R=[a_t.b], W=[a_t.b])
            T.op(T.DVE, lambda h: h.tensor_scalar(out=o1[:], in0=a_t[:], scalar1=nA[:, 0:1], scalar2=None, op0=ALU.mult), R=[a_t.b, nA.b], W=[o1.b])
            T.dma(T.ACT, G.gbT[0:16, t0:t0 + TG], o1[:], R=[o1.b], W=[G.dbuf("gbT")], sb=o1.b)
            T.op(T.ACT, lambda h: h.activation(out=b_t[:], in_=b_t[:], func=AF.Exp, scale=-1.0), R=[b_t.b], W=[b_t.b])
            T.op(T.DVE, lambda h: h.tensor_scalar(out=b_t[:], in0=b_t[:], scalar1=1.0, scalar2=None, op0=ALU.add), R=[b_t.b], W=[b_t.b])
            T.op(T.DVE, lambda h: h.reciprocal(out=o2[:], in_=b_t[:]), R=[b_t.b], W=[o2.b])
            T.dma(T.ACT, G.gbT[16:32, t0:t0 + TG], o2[:], R=[o2.b], W=[G.dbuf("gbT")], sb=o2.b)


def bc_h(ap2d):
    return ap2d.unsqueeze(1).to_broadcast([128, 8, 128])


def bc_x(ap_h):
    return ap_h.unsqueeze(2).to_broadcast([128, 8, 128])


def v3(ap):
    return ap.rearrange("p (h x) -> p h x", h=8)


def phase_dn1(G, l):
    T, nc, NTOK = G.T, G.nc, G.NTOK
    PS, PB = G.PS, G.PB
    ST = 512 if NTOK >= 512 else NTOK
    NST = NTOK // ST
    identf = G.masks[:, M_IDENT, :]
    with Phase(G, "dn1") as ph:
        qS = [ph.sb(f"qS{i}", [128, 8, ST], BF16, dma=True) for i in range(2)]
        kS = [ph.sb(f"kS{i}", [128, 8, ST], BF16, dma=True) for i in range(2)]
        vS = [ph.sb(f"vS{i}", [128, 8, ST], BF16, dma=True) for i in range(2)]
        gS = [ph.sb(f"gS{i}", [32, ST], F32, dma=True) for i in range(2)]
        ktm = ph.sb("ktm", [128, 1024], BF16)
        vtm = ph.sb("vtm", [128, 1024], BF16)
        qtm = ph.sb("qtm", [128, 1024], BF16)
        gt = ph.sb("gt", [128, 32], F32)
        QKs = ph.sb("QKs", [128, 1024], F32)
        KKs = ph.sb("KKs", [128, 1024], F32)

        class D:
            pass
        dd = []
        for d in range(2):
            o = D()
            o.ex = ph.sb(f"ex{d}", [128, 32], F32, dma=True)
            o.bg = ph.sb(f"bg{d}", [128, 8], F32)
            o.Gm = ph.sb(f"Gm{d}", [128, 1024], F32)
            o.E = ph.sb(f"E{d}", [128, 1024], F32)
            o.EL = ph.sb(f"EL{d}", [128, 1024], F32)
            o.aqk = ph.sb(f"aqk{d}", [128, 1024], BF16)
            o.aT = ph.sb(f"aT{d}", [128, 1024], BF16, dma=True)
            o.X = [ph.sb(f"X{d}_{i}", [128, 1024], BF16) for i in range(2)]
            o.XT = [ph.sb(f"XT{d}_{i}", [128, 1024], BF16) for i in range(2)]
            o.RT = [ph.sb(f"RT{d}_{i}", [128, 1024], BF16) for i in range(2)]
            o.kbg = ph.sb(f"kbg{d}", [128, 1024], BF16)
            o.vb = ph.sb(f"vb{d}", [128, 1024], BF16)
            o.kd = ph.sb(f"kd{d}", [128, 1024], BF16, dma=True)
            o.qgm = ph.sb(f"qgm{d}", [128, 1024], BF16)
            o.qgT = ph.sb(f"qgT{d}", [128, 1024], BF16, dma=True)
            o.wT = ph.sb(f"wT{d}", [128, 1024], BF16, dma=True)
            o.u = ph.sb(f"u{d}", [128, 1024], F32, dma=True)
            dd.append(o)
        pcnt = [0]

        def ps2():
            i = pcnt[0] % 4
            pcnt[0] += 1
            return PS[i], [PB[2 * i], PB[2 * i + 1]]

        def load_st(st_):
            s0 = st_ * ST
            sl = st_ % 2
            for (ring, which) in ((qS, 0), (kS, 1), (vS, 2)):
                T.dma(T.SP, ring[sl][:], G.qkvT[which * 1024:(which + 1) * 1024, s0:s0 + ST].rearrange("(h d) t -> d h t", d=128),
                      R=[G.dbuf("qkvT")], W=[ring[sl].b], sb=ring[sl].b)
            T.dma(T.SP, gS[sl][:], G.gbT[:, s0:s0 + ST], R=[G.dbuf("gbT")], W=[gS[sl].b], sb=gS[sl].b)

        load_st(0)
        for st_ in range(NST):
            s0 = st_ * ST
            sl = st_ % 2
            if st_ + 1 < NST:
                load_st(st_ + 1)
            q_, k_, v_, g_ = qS[sl], kS[sl], vS[sl], gS[sl]
            for j in range(ST // 128):
                c0 = j * 128
                tile = (s0 + c0) // 128
                for (src, dst) in ((k_, ktm), (v_, vtm), (q_, qtm)):
                    p, pb = ps2()
                    pbf = p[:, :].bitcast(BF16)
                    for h_ in range(8):
                        T.op(T.PE, lambda h: h.transpose(pbf[:, h_ * 128:(h_ + 1) * 128], src[:, h_, c0:c0 + 128], G.identb[:]),
                             R=[src.b, G.identb.b], W=[pb[0]])
                    T.op(T.ACT, lambda h: h.copy(out=dst[:], in_=pbf[:, 0:1024]), R=[pb[0]], W=[dst.b])
                p, pb = ps2()
                T.op(T.PE, lambda h: h.transpose(p[:, 0:32], g_[:, c0:c0 + 128], identf[0:32, 0:32]), R=[g_.b, G.masks.b], W=[pb[0]])
                T.op(T.DVE, lambda h: h.tensor_copy(out=gt[:], in_=p[:, 0:32]), R=[pb[0]], W=[gt.b])
                for (lh, dst) in ((q_, QKs), (k_, KKs)):
                    p, pb = ps2()
                    for h_ in range(8):
                        T.op(T.PE, lambda h: h.matmul(p[:, h_ * 128:(h_ + 1) * 128], lhsT=lh[:, h_, c0:c0 + 128], rhs=k_[:, h_, c0:c0 + 128], start=True, stop=True),
                             R=[lh.b, k_.b], W=[pb[h_ // 4]])
                    T.op(T.ACT, lambda h: h.copy(out=dst[:], in_=p[:, :]), R=pb, W=[dst.b])
                stg = [None, None]
                for d in range(2):
                    o = dd[d]
                    gd = gt[:, d * 8:(d + 1) * 8]
                    bd = gt[:, 16 + d * 8:16 + (d + 1) * 8]
                    p, pb = ps2()
                    for ci, mk in enumerate((M_U[d], M_BU[d], M_SEL0, M_SEL1)):
                        T.op(T.PE, lambda h: h.matmul(p[:, ci * 8:(ci + 1) * 8], lhsT=G.masks[:, mk, :], rhs=gd, start=True, stop=True),
                             R=[G.masks.b, gt.b], W=[pb[0]])
                    T.op(T.ACT, lambda h: h.activation(out=o.ex[:], in_=p[:, 0:32], func=AF.Exp), R=[pb[0]], W=[o.ex.b])
                    T.dma(T.ACT, G.dn_sc[d][tile, :, :], o.ex[:], R=[o.ex.b], W=[G.dbuf(f"dn_sc{d}")], sb=o.ex.b)
                    T.op(T.DVE, lambda h: h.tensor_tensor(out=o.bg[:], in0=bd, in1=o.ex[:, 0:8], op=ALU.mult), R=[gt.b, o.ex.b], W=[o.bg.b])
                    T.op(T.POOL, lambda h: h.tensor_tensor(out=v3(o.Gm[:, :]), in0=bc_h(G.masks[:, M_U[d], :]), in1=bc_x(gd), op=ALU.mult),
                         R=[G.masks.b, gt.b], W=[o.Gm.b])
                    p, pb = ps2()
                    for h_ in range(8):
                        T.op(T.PE, lambda h: h.matmul(p[:, h_ * 128:(h_ + 1) * 128], lhsT=o.Gm[:, h_ * 128:(h_ + 1) * 128], rhs=G.masks[:, M_LC[d], :], start=True, stop=True),
                             R=[o.Gm.b, G.masks.b], W=[pb[h_ // 4]])
                    T.op(T.ACT, lambda h: h.activation(out=o.E[:], in_=p[:, :], func=AF.Exp), R=pb, W=[o.E.b])
                    T.op(T.POOL, lambda h: h.tensor_tensor(out=v3(o.EL[:, :]), in0=v3(o.E[:, :]), in1=bc_h(G.masks[:, M_INCL[d], :]), op=ALU.mult),
                         R=[o.E.b, G.masks.b], W=[o.EL.b])
                    T.op(T.POOL, lambda h: h.tensor_tensor(out=v3(o.E[:, :]), in0=v3(o.E[:, :]), in1=bc_h(G.masks[:, M_NSTRICT[d], :]), op=ALU.mult),
                         R=[o.E.b, G.masks.b], W=[o.E.b])
                    T.op(T.POOL, lambda h: h.tensor_tensor(out=v3(o.E[:, :]), in0=v3(o.E[:, :]), in1=bc_x(bd), op=ALU.mult),
                         R=[o.E.b, gt.b], W=[o.E.b])
                    T.op(T.DVE, lambda h: h.tensor_tensor(out=o.aqk[:], in0=QKs[:], in1=o.EL[:], op=ALU.mult), R=[QKs.b, o.EL.b], W=[o.aqk.b])
                    T.op(T.DVE, lambda h: h.tensor_tensor(out=o.X[0][:], in0=KKs[:], in1=o.E[:], op=ALU.mult), R=[KKs.b, o.E.b], W=[o.X[0].b])
                    T.op(T.POOL, lambda h: h.tensor_tensor(out=v3(o.kbg[:, :]), in0=v3(ktm[:, :]), in1=bc_x(o.bg[:, :]), op=ALU.mult),
                         R=[ktm.b, o.bg.b], W=[o.kbg.b])
                    T.op(T.POOL, lambda h: h.tensor_tensor(out=v3(o.vb[:, :]), in0=v3(vtm[:, :]), in1=bc_x(bd), op=ALU.mult),
                         R=[vtm.b, gt.b], W=[o.vb.b])
                    T.op(T.POOL, lambda h: h.tensor_tensor(out=v3(o.kd[:, :]), in0=v3(ktm[:, :]), in1=bc_x(o.ex[:, 8:16]), op=ALU.mult),
                         R=[ktm.b, o.ex.b], W=[o.kd.b])
                    T.dma(T.SP, G.dn_kd[d][tile, :, :], o.kd[:], R=[o.kd.b], W=[G.dbuf(f"dn_kd{d}")], sb=o.kd.b)
                    T.op(T.POOL, lambda h: h.tensor_tensor(out=v3(o.qgm[:, :]), in0=v3(qtm[:, :]), in1=bc_x(o.ex[:, 0:8]), op=ALU.mult),
                         R=[qtm.b, o.ex.b], W=[o.qgm.b])
                for d in range(2):
                    o = dd[d]
                    p, pb = ps2()
                    pbf = p[:, :].bitcast(BF16)
                    for h_ in range(8):
                        T.op(T.PE, lambda h: h.transpose(pbf[:, h_ * 128:(h_ + 1) * 128], o.aqk[:, h_ * 128:(h_ + 1) * 128], G.identb[:]),
                             R=[o.aqk.b, G.identb.b], W=[pb[0]])
                    for h_ in range(8):
                        T.op(T.PE, lambda h: h.transpose(pbf[:, 1024 + h_ * 128:1024 + (h_ + 1) * 128], o.X[0][:, h_ * 128:(h_ + 1) * 128], G.identb[:]),
                             R=[o.X[0].b, G.identb.b], W=[pb[1]])
                    T.op(T.ACT, lambda h: h.copy(out=o.aT[:], in_=pbf[:, 0:1024]), R=[pb[0]], W=[o.aT.b])
                    T.dma(T.SP, G.dn_aT[d][tile, :, :], o.aT[:], R=[o.aT.b], W=[G.dbuf(f"dn_aT{d}")], sb=o.aT.b)
                    T.op(T.DVE, lambda h: h.tensor_copy(out=o.XT[0][:], in_=pbf[:, 1024:2048]), R=[pb[1]], W=[o.XT[0].b])
                    T.op(T.POOL, lambda h: h.tensor_tensor(out=v3(o.RT[0][:, :]), in0=v3(o.XT[0][:, :]), in1=bc_h(G.identb[:]), op=ALU.add),
                         R=[o.XT[0].b, G.identb.b], W=[o.RT[0].b])
                    p, pb = ps2()
                    pbf = p[:, :].bitcast(BF16)
                    for h_ in range(8):
                        T.op(T.PE, lambda h: h.transpose(pbf[:, h_ * 128:(h_ + 1) * 128], o.qgm[:, h_ * 128:(h_ + 1) * 128], G.identb[:]),
                             R=[o.qgm.b, G.identb.b], W=[pb[0]])
                    T.op(T.ACT, lambda h: h.copy(out=o.qgT[:], in_=pbf[:, 0:1024]), R=[pb[0]], W=[o.qgT.b])
                    T.dma(T.SP, G.dn_qg[d][tile, :, :], o.qgT[:], R=[o.qgT.b], W=[G.dbuf(f"dn_qg{d}")], sb=o.qgT.b)
                for k in range(1, 6):
                    a, b = (k - 1) % 2, k % 2
                    for d in range(2):
                        o = dd[d]
                        p, pb = ps2()
                        for h_ in range(8):
                            hs = slice(h_ * 128, (h_ + 1) * 128)
                            T.op(T.PE, lambda h: h.matmul(p[:, hs], lhsT=o.XT[a][:, hs], rhs=o.X[a][:, hs], start=True, stop=True),
                                 R=[o.XT[a].b, o.X[a].b], W=[pb[h_ // 4]])
                        T.op(T.ACT, lambda h: h.copy(out=o.X[b][:], in_=p[:, :]), R=pb, W=[o.X[b].b])
                        if k < 5:
                            p, pb = ps2()
                            for h_ in range(8):
                                hs = slice(h_ * 128, (h_ + 1) * 128)
                                T.op(T.PE, lambda h: h.matmul(p[:, hs], lhsT=o.X[a][:, hs], rhs=o.XT[a][:, hs], start=True, stop=True),
                                     R=[o.XT[a].b, o.X[a].b], W=[pb[h_ // 4]])
                            T.op(T.DVE, lambda h: h.tensor_copy(out=o.XT[b][:], in_=p[:, :]), R=pb, W=[o.XT[b].b])
                    for d in range(2):
                        o = dd[d]
                        p, pb = ps2()
                        for h_ in range(8):
                            hs = slice(h_ * 128, (h_ + 1) * 128)
                            T.op(T.PE, lambda h: h.matmul(p[:, hs], lhsT=o.X[b][:, hs], rhs=o.RT[a][:, hs], start=True, stop=True),
                                 R=[o.X[b].b, o.RT[a].b], W=[pb[h_ // 4]])
                        T.op(T.DVE, lambda h: h.tensor_tensor(out=o.RT[b][:], in0=p[:, :], in1=o.RT[a][:], op=ALU.add), R=pb + [o.RT[a].b], W=[o.RT[b].b])
                for d in range(2):
                    o = dd[d]
                    TT = o.RT[1]
                    p, pb = ps2()
                    for h_ in range(8):
                        hs = slice(h_ * 128, (h_ + 1) * 128)
                        T.op(T.PE, lambda h: h.matmul(p[:, hs], lhsT=o.kbg[:, hs], rhs=TT[:, hs], start=True, stop=True),
                             R=[o.kbg.b, TT.b], W=[pb[h_ // 4]])
                    T.op(T.ACT, lambda h: h.copy(out=o.wT[:], in_=p[:, :]), R=pb, W=[o.wT.b])
                    T.dma(T.SP, G.dn_wT[d][tile, :, :], o.wT[:], R=[o.wT.b], W=[G.dbuf(f"dn_wT{d}")], sb=o.wT.b)
                    p, pb = ps2()
                    for h_ in range(8):
                        hs = slice(h_ * 128, (h_ + 1) * 128)
                        T.op(T.PE, lambda h: h.matmul(p[:, hs], lhsT=TT[:, hs], rhs=o.vb[:, hs], start=True, stop=True),
                             R=[o.vb.b, TT.b], W=[pb[h_ // 4]])
                    T.op(T.DVE, lambda h: h.tensor_copy(out=o.u[:], in_=p[:, :]), R=pb, W=[o.u.b])
                    T.dma(T.SP, G.dn_u[d][tile, :, :], o.u[:], R=[o.u.b], W=[G.dbuf(f"dn_u{d}")], sb=o.u.b)


def phase_dn2(G, l):
    T, nc, NTOK = G.T, G.nc, G.NTOK
    PS, PB = G.PS, G.PB
    NTL = NTOK // 128
    NCHK = NTOK // CH
    with Phase(G, "dn2") as ph:
        class D:
            pass
        dd = []
        for d in range(2):
            o = D()
            o.wT = [ph.sb(f"wT{d}_{i}", [128, 1024], BF16, dma=True) for i in range(2)]
            o.qg = [ph.sb(f"qg{d}_{i}", [128, 1024], BF16, dma=True) for i in range(2)]
            o.u2 = [ph.sb(f"u2{d}_{i}", [64, 2, 1024], F32, dma=True) for i in range(2)]
            o.a2 = [ph.sb(f"a2{d}_{i}", [64, 2, 1024], BF16, dma=True) for i in range(2)]
            o.k2 = [ph.sb(f"k2{d}_{i}", [64, 2, 1024], BF16, dma=True) for i in range(2)]
            o.gl = [ph.sb(f"gl{d}_{i}", [128, 16], F32, dma=True) for i in range(2)]
            o.o2 = [ph.sb(f"o2{d}_{i}", [128, 2, 1024], F32, dma=True) for i in range(2)]
            o.S = ph.sb(f"S{d}", [128, 1024], F32)
            o.Sb = ph.sb(f"Sb{d}", [128, 1024], BF16)
            o.Sd = ph.sb(f"Sd{d}", [128, 1024], F32)
            o.vn = ph.sb(f"vn{d}", [64, 1024], BF16)
            o.pa, o.pab = PS[2 * d], [PB[4 * d], PB[4 * d + 1]]
            o.pk, o.pkb = PS[2 * d + 1], [PB[4 * d + 2], PB[4 * d + 3]]
            T.op(T.DVE, lambda h: h.memset(o.S[:], 0.0), W=[o.S.b])
            T.op(T.DVE, lambda h: h.memset(o.Sb[:], 0.0), W=[o.Sb.b])
            dd.append(o)

        def tile_of(d, n):
            return n if d == 0 else NTL - 1 - n

        def load_tile(d, n):
            o = dd[d]
            t = tile_of(d, n)
            s = n % 2
            T.dma(T.SP, o.wT[s][:], G.dn_wT[d][t, :, :], R=[G.dbuf(f"dn_wT{d}")], W=[o.wT[s].b], sb=o.wT[s].b)
            T.dma(T.SP, o.qg[s][:], G.dn_qg[d][t, :, :], R=[G.dbuf(f"dn_qg{d}")], W=[o.qg[s].b], sb=o.qg[s].b)
            T.dma(T.SP, o.u2[s][:], G.dn_u[d][t, :, :].rearrange("(c p) f -> p c f", p=64), R=[G.dbuf(f"dn_u{d}")], W=[o.u2[s].b], sb=o.u2[s].b)
            T.dma(T.SP, o.a2[s][:], G.dn_aT[d][t, :, :].rearrange("(c p) f -> p c f", p=64), R=[G.dbuf(f"dn_aT{d}")], W=[o.a2[s].b], sb=o.a2[s].b)
            T.dma(T.SP, o.k2[s][:], G.dn_kd[d][t, :, :].rearrange("(c p) f -> p c f", p=64), R=[G.dbuf(f"dn_kd{d}")], W=[o.k2[s].b], sb=o.k2[s].b)
            T.dma(T.SP, o.gl[s][:], G.dn_sc[d][t, :, 16:32], R=[G.dbuf(f"dn_sc{d}")], W=[o.gl[s].b], sb=o.gl[s].b)

        for d in range(2):
            load_tile(d, 0)
        for n in range(NTL):
            for d in range(2):
                if n + 1 < NTL:
                    load_tile(d, n + 1)
            for cc in range(2):
                for d in range(2):
                    o = dd[d]
                    s = n % 2
                    c = cc if d == 0 else 1 - cc
                    t = tile_of(d, n)
                    chunk = t * 2 + c
                    cs = slice(c * 64, c * 64 + 64)
                    if (d == 0 and chunk == NCHK // 2) or (d == 1 and chunk == NCHK // 2 - 1):
                        T.op(T.DVE, lambda h: h.tensor_scalar(out=o.S[:], in0=o.S[:], scalar1=G.flags[:, 0:1], scalar2=None, op0=ALU.mult),
                             R=[o.S.b, G.flags.b], W=[o.S.b])
                        T.op(T.ACT, lambda h: h.copy(out=o.Sb[:], in_=o.S[:]), R=[o.S.b], W=[o.Sb.b])
                    wT3, qg3 = v3(o.wT[s][:, :]), v3(o.qg[s][:, :])
                    for h_ in range(8):
                        hs = slice(h_ * 128, (h_ + 1) * 128)
                        T.op(T.PE, lambda h: h.matmul(o.pa[0:64, hs], lhsT=wT3[:, h_, cs], rhs=o.Sb[:, hs], start=True, stop=True),
                             R=[o.wT[s].b, o.Sb.b], W=[o.pab[h_ // 4]])
                    T.op(T.DVE, lambda h: h.tensor_tensor(out=o.vn[:], in0=o.u2[s][:, c, :], in1=o.pa[0:64, :], op=ALU.subtract),
                         R=[o.u2[s].b] + o.pab, W=[o.vn.b])
                    T.op(T.POOL, lambda h: h.tensor_tensor(out=v3(o.Sd[:, :]), in0=v3(o.S[:, :]), in1=bc_x(o.gl[s][:, c * 8:(c + 1) * 8]), op=ALU.mult),
                         R=[o.S.b, o.gl[s].b], W=[o.Sd.b])
                    for h_ in range(8):
                        hs = slice(h_ * 128, (h_ + 1) * 128)
                        T.op(T.PE, lambda h: h.matmul(o.pk[:, hs], lhsT=o.k2[s][:, c, hs], rhs=o.vn[:, hs], start=True, stop=True),
                             R=[o.k2[s].b, o.vn.b], W=[o.pkb[h_ // 4]])
                    for h_ in range(8):
                        hs = slice(h_ * 128, (h_ + 1) * 128)
                        T.op(T.PE, lambda h: h.matmul(o.pa[64:128, hs], lhsT=qg3[:, h_, cs], rhs=o.Sb[:, hs], start=True, stop=False),
                             R=[o.qg[s].b, o.Sb.b], W=[o.pab[h_ // 4]])
                        T.op(T.PE, lambda h: h.matmul(o.pa[64:128, hs], lhsT=o.a2[s][:, c, h_ * 128 + c * 64:h_ * 128 + c * 64 + 64], rhs=o.vn[:, hs],
                                                      start=False, stop=True),
                             R=[o.a2[s].b, o.vn.b], W=[o.pab[h_ // 4]])
                    T.op(T.DVE, lambda h: h.tensor_tensor(out=o.S[:], in0=o.Sd[:], in1=o.pk[:, :], op=ALU.add), R=[o.Sd.b] + o.pkb, W=[o.S.b])
                    T.op(T.ACT, lambda h: h.copy(out=o.Sb[:], in_=o.S[:]), R=[o.S.b], W=[o.Sb.b])
                    T.op(T.ACT, lambda h: h.copy(out=o.o2[s][64:128, c, :], in_=o.pa[64:128, :]), R=o.pab, W=[o.o2[s].b])
            for d in range(2):
                o = dd[d]
                s = n % 2
                t = tile_of(d, n)
                T.dma(T.ACT, G.dn_o[d][t * 128:(t + 1) * 128, :].rearrange("(c p) f -> p c f", p=64), o.o2[s][64:128, :, :],
                      R=[o.o2[s].b], W=[G.dbuf(f"dn_o{d}")], sb=o.o2[s].b)


def phase_dn3(G, l):
    T, nc, NTOK = G.T, G.nc, G.NTOK
    PS, PB = G.PS, G.PB
    NTL = NTOK // 128
    with Phase(G, "dn3") as ph:
        of = [ph.sb(f"of{i}", [128, 1024], F32, dma=True) for i in range(2)]
        obw = [ph.sb(f"obw{i}", [128, 1024], F32, dma=True) for i in range(2)]
        sz = [ph.sb(f"sz{i}", [128, 8, 128], BF16, dma=True) for i in range(2)]
        osum = ph.sb("osum", [128, 1024], F32)
        sqt = ph.sb("sqt", [128, 1024], F32)
        ss = ph.sb("ss", [128, 8], F32)
        on = ph.sb("on", [128, 1024], BF16)
        dnw = ph.sb("dnw", [128, 128], F32, dma=True)
        oo = [ph.sb(f"oo{i}", [128, 8, 128], BF16, dma=True) for i in range(2)]
        T.dma(T.SP, dnw[:], G.dnw_d[l, :].partition_broadcast(128), W=[dnw.b], sb=dnw.b)
        for t in range(NTL):
            s = t % 2
            T.dma(T.SP, of[s][:], G.dn_o[0][t * 128:(t + 1) * 128, :], R=[G.dbuf("dn_o0")], W=[of[s].b], sb=of[s].b)
            T.dma(T.SP, obw[s][:], G.dn_o[1][t * 128:(t + 1) * 128, :], R=[G.dbuf("dn_o1")], W=[obw[s].b], sb=obw[s].b)
            T.dma(T.SP, sz[s][:], G.szT[:, t * 128:(t + 1) * 128].rearrange("(h e) t -> e h t", e=128), R=[G.dbuf("szT")], W=[sz[s].b], sb=sz[s].b)
            T.op(T.POOL, lambda h: h.tensor_tensor(out=osum[:], in0=of[s][:], in1=obw[s][:], op=ALU.add), R=[of[s].b, obw[s].b], W=[osum.b])
            T.op(T.ACT, lambda h: h.activation(out=sqt[:], in_=osum[:], func=AF.Square), R=[osum.b], W=[sqt.b])
            T.op(T.DVE, lambda h: h.tensor_reduce(out=ss[:], in_=v3(sqt[:, :]), axis=AX.X, op=ALU.add), R=[sqt.b], W=[ss.b])
            T.op(T.ACT, lambda h: h.activation(out=ss[:], in_=ss[:], func=AF.Ln, scale=1.0 / 128, bias=NORM_EPS), R=[ss.b], W=[ss.b])
            T.op(T.ACT, lambda h: h.activation(out=ss[:], in_=ss[:], func=AF.Exp, scale=-0.5), R=[ss.b], W=[ss.b])
            T.op(T.DVE, lambda h: h.tensor_tensor(out=v3(osum[:, :]), in0=v3(osum[:, :]), in1=bc_x(ss[:, :]), op=ALU.mult), R=[osum.b, ss.b], W=[osum.b])
            T.op(T.POOL, lambda h: h.tensor_tensor(out=v3(on[:, :]), in0=v3(osum[:, :]), in1=bc_h(dnw[:]), op=ALU.mult), R=[osum.b, dnw.b], W=[on.b])
            pi = t % 4
            p, pb = PS[pi], [PB[2 * pi], PB[2 * pi + 1]]
            pbf = p[:, :].bitcast(BF16)
            for h_ in range(8):
                T.op(T.PE, lambda h: h.transpose(pbf[:, h_ * 128:(h_ + 1) * 128], on[:, h_ * 128:(h_ + 1) * 128], G.identb[:]),
                     R=[on.b, G.identb.b], W=[pb[0]])
            o = oo[s]
            T.op(T.DVE, lambda h: h.tensor_tensor(out=o[:].rearrange("p h t -> p (h t)"), in0=pbf[:, 0:1024], in1=sz[s][:].rearrange("p h t -> p (h t)"), op=ALU.mult),
                 R=[pb[0], sz[s].b], W=[o.b])
            T.dma(T.ACT, G.mixT[0:1024, t * 128:(t + 1) * 128].rearrange("(h e) t -> e h t", e=128), o[:], R=[o.b], W=[G.dbuf("mixT")], sb=o.b)


def phase_e2(G, l):
    T, nc, NTOK = G.T, G.nc, G.NTOK
    PB = G.PB
    TBE = 512
    NBE = NTOK // TBE
    with Phase(G, "e2") as ph:
        diag = ph.sb("diag", [128, 3, 128], BF16)
        ain = [ph.sb(f"ain{i}", [128, TBE + 2], BF16, dma=True) for i in range(3)]
        bin_ = [ph.sb(f"bin{i}", [128, TBE], BF16, dma=True) for i in range(3)]
        sl = [ph.sb(f"sl{i}", [128, TBE], F32) for i in range(2)]
        ob = [ph.sb(f"ob{i}", [128, TBE], BF16, dma=True) for i in range(3)]
        cnt = 0
        for f in range(44):
            make_diags(G, diag, l, C_CFFN, f)
            for blk in range(NBE):
                t0 = blk * TBE
                a, b, s_, o = ain[cnt % 3], bin_[cnt % 3], sl[cnt % 2], ob[cnt % 3]
                bi = cnt % 4
                cnt += 1
                T.dma(T.SP, a[:], G.upA[f * 128:(f + 1) * 128, t0:t0 + TBE + 2], R=[G.dbuf("upA")], W=[a.b], sb=a.b)
                halo_fix(G, a, t0, TBE, TBE + 2)
                T.dma(T.SP, b[:], G.upB[f * 128:(f + 1) * 128, t0:t0 + TBE], R=[G.dbuf("upB")], W=[b.b], sb=b.b)
                for tap in range(3):
                    T.op(T.PE, lambda h: h.matmul(bank_ap(G, bi), lhsT=diag[:, tap, :], rhs=a[:, tap:tap + TBE], start=(tap == 0), stop=(tap == 2)),
                         R=[diag.b, a.b], W=[PB[bi]])
                T.op(T.ACT, lambda h: h.activation(out=s_[:], in_=bank_ap(G, bi), func=AF.Silu), R=[PB[bi]], W=[s_.b])
                T.op(T.DVE, lambda h: h.tensor_tensor(out=o[:], in0=s_[:], in1=b[:], op=ALU.mult), R=[s_.b, b.b], W=[o.b])
                T.dma(T.ACT, G.gfT[f * 128:(f + 1) * 128, t0:t0 + TBE], o[:], R=[o.b], W=[G.dbuf("gfT")], sb=o.b)


def phase_init(G):
    T = G.T
    NTOK = G.NTOK
    with Phase(G, "init") as ph:
        z = ph.sb("z", [128, 64], BF16, dma=True)
        T.op(T.DVE, lambda h: h.memset(z[:], 0.0), W=[z.b])
        with G.nc.allow_non_contiguous_dma(reason="one-time pad column init"):
            for (ten, nm, rows) in ((G.pT, "pT", 7168), (G.upA, "upA", D_FF)):
                n = rows // 128
                for col in (0, NTOK + 1):
                    T.dma(T.SP, ten[:, col:col + 1].rearrange("(n p) o -> p n o", p=128), z[:, 0:n].unsqueeze(2), R=[z.b], W=[G.dbuf(nm)], sb=z.b)


NTOK_CORE = 16384
_PROG = {}


def _get_prog():
    if "G" not in _PROG:
        _PROG["G"] = build_program(NTOK_CORE, DEPTH)
    return _PROG["G"]


def kernel(x_prompt, x_sample, norm_mix_pre, w_in, conv_qkv, a_log, dt_bias, dn_norm, conv_sc, sc_norm,
           w_out, norm_mix_post, norm_ffn_pre, w_up, conv_ffn, w_down, norm_ffn_post):
    f32 = np.float32
    xp = np.asarray(x_prompt, f32)
    xs = np.asarray(x_sample, f32)
    inp = dict(norm_mix_pre=norm_mix_pre, norm_mix_post=norm_mix_post, norm_ffn_pre=norm_ffn_pre, norm_ffn_post=norm_ffn_post,
               sc_norm=sc_norm, conv_qkv=conv_qkv, conv_sc=conv_sc, conv_ffn=conv_ffn, a_log=a_log, dt_bias=dt_bias, dn_norm=dn_norm)
    inp = {k: np.asarray(v, f32) for k, v in inp.items()}
    cols, gatep, dnw = prep_small(inp, DEPTH)
    G = _get_prog()
    masks = host_consts()
    w_in_, w_out_, w_up_, w_down_ = (np.ascontiguousarray(np.asarray(a, f32)) for a in (w_in, w_out, w_up, w_down))
    seqs = [xs[0], xs[1], np.concatenate([xp[0], xp[1]], axis=0)]
    flags = [1.0, 1.0, 0.0]
    zero_xT = np.zeros((D_MODEL, NTOK_CORE), f32)
    in_maps = []
    for c in range(8):
        if c < 3:
            xT = np.ascontiguousarray(seqs[c].T)
            fl = np.full((128, 4), flags[c], f32)
        else:
            xT = zero_xT
            fl = np.ones((128, 4), f32)
        in_maps.append(dict(xT=xT, w_in=w_in_, w_out=w_out_, w_up=w_up_, w_down=w_down_, cols=cols, gatep=gatep, dnw=dnw,
                            masks=masks, flags=fl))
    res = run_bass_kernel_spmd(G.nc, in_maps, core_ids=list(range(8)))
    ys = [np.asarray(res.results[c]["yT"], f32).T for c in range(3)]
    y_sample = np.stack([ys[0], ys[1]], axis=0)
    y_prompt = np.stack([ys[2][:8192], ys[2][8192:]], axis=0)
    return (np.ascontiguousarray(y_prompt), np.ascontiguousarray(y_sample))
```

```python
import numpy as np
import ml_dtypes
import concourse.bass as bass
import concourse.mybir as mybir
from concourse.bass_utils import run_bass_kernel_spmd

F32 = mybir.dt.float32
BF16 = mybir.dt.bfloat16
AF = mybir.ActivationFunctionType
ALU = mybir.AluOpType
AX = mybir.AxisListType

D_MODEL = 2048
DEPTH = 4
DN_HEADS = 8
HD = 128
DN_DIM = 1024
SC_DIM = 1024
D_FF = 5632
NORM_EPS = 1e-6
L2_EPS = 1e-6
CH = 64


class Buf:
    __slots__ = ("name", "writers", "readers", "state", "dsem", "dcount")

    def __init__(self, name):
        self.name = name
        self.writers = {}
        self.readers = {}
        self.state = 0
        self.dsem = None
        self.dcount = 0


class Eng:
    def __init__(self, nc, h, name, pe=False):
        self.h = h
        self.name = name
        self.sem = nc.alloc_semaphore("c_" + name)
        self.n = 0
        self.waited = {}
        self.pe = pe


class Trk:
    def __init__(self, nc):
        self.nc = nc
        self.PE = Eng(nc, nc.tensor, "pe", pe=True)
        self.ACT = Eng(nc, nc.scalar, "act")
        self.DVE = Eng(nc, nc.vector, "dve")
        self.POOL = Eng(nc, nc.gpsimd, "pool")
        self.SP = Eng(nc, nc.sync, "sp")
        self.engs = [self.PE, self.ACT, self.DVE, self.POOL, self.SP]
        self.dma_sems = []
        self.nbuf = 0

    def buf(self, name=None):
        self.nbuf += 1
        return Buf(name or f"b{self.nbuf}")

    def _wait(self, e, deps):
        for sem, val in deps.items():
            if e.waited.get(sem, 0) < val:
                e.h.wait_ge(sem, val)
                e.waited[sem] = val

    def _collect(self, e, R, W):
        deps = {}

        def add(d):
            for s, v in d.items():
                if e.pe and s is e.sem:
                    continue
                if deps.get(s, 0) < v:
                    deps[s] = v
        for b in R:
            add(b.writers)
        for b in W:
            if b.state == 1:
                add(b.readers)
            else:
                add(b.writers)
                add(b.readers)
        return deps

    def _record(self, R, W, sem, val):
        for b in R:
            if b.state == 0:
                b.state = 1
            if b.readers.get(sem, 0) < val:
                b.readers[sem] = val
        for b in W:
            if b.state == 1:
                b.state = 0
                b.readers = {}
                b.writers = {}
            if b.writers.get(sem, 0) < val:
                b.writers[sem] = val

    def op(self, e, fn, R=(), W=(), inc=True):
        inc = True
        self._wait(e, self._collect(e, R, W))
        ins = fn(e.h)
        if inc:
            e.n += 1
            ins.then_inc(e.sem, 1)
            val = e.n
            e.pending = False
        else:
            assert e.pe
            val = e.n + 1
            e.pending = True
        self._record(R, W, e.sem, val)
        return ins

    def dma(self, e, out, in_, R=(), W=(), sb=None, **kw):
        self._wait(e, self._collect(e, R, W))
        if sb.dsem is None:
            sb.dsem = self.nc.alloc_semaphore("d_" + sb.name)
            self.dma_sems.append(sb)
        ins = e.h.dma_start(out=out, in_=in_, **kw)
        sb.dcount += 16
        ins.then_inc(sb.dsem, 16)
        self._record(R, W, sb.dsem, sb.dcount)
        return ins

    def barrier(self):
        assert not getattr(self.PE, "pending", False)
        deps = {}
        for e in self.engs:
            if e.n:
                deps[e.sem] = e.n
        for b in self.dma_sems:
            if b.dcount:
                deps[b.dsem] = b.dcount
        for e in self.engs:
            d = {s: v for s, v in deps.items() if not (s is e.sem)}
            self._wait(e, d)

    def free_dma_sems(self, bufs):
        for b in bufs:
            if b.dsem is not None:
                self.dma_sems.remove(b)


class Tl:
    def __init__(self, t, b):
        self.t = t
        self.b = b

    def __getitem__(self, k):
        return self.t[k]


from contextlib import ExitStack

NCOLS_L = 16 * 4 + 8 + 72 + 24 + 132
C_MIXPRE, C_MIXPOST, C_FFNPRE, C_FFNPOST, C_SCN, C_CQKV, C_CSC, C_CFFN = 0, 16, 32, 48, 64, 72, 144, 168

M_IDENT, M_ONES = 0, 1
M_U = (2, 3)
M_LC = (4, 5)
M_INCL = (6, 7)
M_NSTRICT = (8, 9)
M_BU = (10, 11)
M_SEL0, M_SEL1 = 12, 13
NMASK = 14


def host_consts():
    m = np.zeros((NMASK, 128, 128), np.float32)
    idx = np.arange(128)
    a = idx[:, None]
    b = idx[None, :]
    same = (a // CH) == (b // CH)
    m[M_IDENT] = np.eye(128)
    m[M_ONES] = 1.0
    m[M_U[0]] = (a <= b) & same
    m[M_U[1]] = (a >= b) & same
    m[M_LC[0]] = (a > b) & same
    m[M_LC[1]] = (a < b) & same
    m[M_INCL[0]] = (b <= a) & same
    m[M_INCL[1]] = (b >= a) & same
    m[M_NSTRICT[0]] = -1.0 * ((b < a) & same)
    m[M_NSTRICT[1]] = -1.0 * ((b > a) & same)
    m[M_BU[0]] = (a > b) & same
    m[M_BU[1]] = (a < b) & same
    m[M_SEL0] = (a < CH) * np.ones((128, 128))
    m[M_SEL1] = (a >= CH) * np.ones((128, 128))
    return np.ascontiguousarray(m.transpose(1, 0, 2))


class Ctx:
    pass


def win_tiles():
    t = [(n * 128, 128) for n in range(32)]
    t += [(4128 + n * 128, 128) for n in range(24)]
    t += [(4096, 32)]
    return t


_uid = [0]


class Phase:
    def __init__(self, G, name):
        self.G = G
        _uid[0] += 1
        self.name = f"{name}{_uid[0]}"

    def __enter__(self):
        self.es = ExitStack()
        self.G.dpool_i = 0
        return self

    def sb(self, name, shape, dt, dma=False):
        G = self.G
        t = self.es.enter_context(G.nc.sbuf_tensor(f"{self.name}_{name}", list(shape), dt))
        return Tl(t, G.take_dbuf() if dma else G.T.buf(name))

    def __exit__(self, *a):
        self.G.T.barrier()
        self.es.close()
        return False


def build_base(NTOK, depth=DEPTH, dbg=()):
    nc = bass.Bass("TRN2", target_bir_lowering=False, disable_frame_to_traceback=True)
    T = Trk(nc)
    G = Ctx()
    G.nc, G.T, G.NTOK, G.depth = nc, T, NTOK, depth
    NT128 = NTOK // 128

    def din(name, shape, dt=F32):
        return nc.dram_tensor(name, list(shape), dt, kind="ExternalInput").ap()

    def dscr(name, shape, dt):
        kind = "ExternalOutput" if name in dbg else "Internal"
        return nc.dram_tensor(name, list(shape), dt, kind=kind).ap()

    L = depth
    G.xT = din("xT", [D_MODEL, NTOK])
    G.w_in = din("w_in", [L, D_MODEL, 7200])
    G.w_out = din("w_out", [L, D_MODEL, D_MODEL])
    G.w_up = din("w_up", [L, D_MODEL, 2 * D_FF])
    G.w_down = din("w_down", [L, D_FF, D_MODEL])
    G.cols_d = din("cols", [128, L * NCOLS_L])
    G.gatep_d = din("gatep", [16, L * 2])
    G.dnw_d = din("dnw", [L, 128])
    G.masks_d = din("masks", [128, NMASK, 128])
    G.flags_d = din("flags", [128, 4])
    G.yT = nc.dram_tensor("yT", [D_MODEL, NTOK], F32, kind="ExternalOutput").ap()

    G.wb_in = [dscr(f"wb_in{l}", [57, 128, 16 * 128], BF16) for l in range(L)]
    G.wb_out = [dscr(f"wb_out{l}", [16, 128, 16 * 128], BF16) for l in range(L)]
    G.wb_up = [dscr(f"wb_up{l}", [88, 128, 16 * 128], BF16) for l in range(L)]
    G.wb_down = [dscr(f"wb_down{l}", [16, 128, 44 * 128], BF16) for l in range(L)]
    G.pT = dscr("pT", [7168, NTOK + 2], BF16)
    G.graw = dscr("graw", [32, NTOK], F32)
    G.gbT = dscr("gbT", [32, NTOK], F32)
    G.qkvT = dscr("qkvT", [3072, NTOK], BF16)
    G.szT = dscr("szT", [1024, NTOK], BF16)
    G.mixT = dscr("mixT", [2048, NTOK], BF16)
    G.xmT = dscr("xmT", [D_MODEL, NTOK], F32)
    G.xsT = [dscr(f"xsT{i}", [D_MODEL, NTOK], F32) for i in range(2)]
    G.upA = dscr("upA", [D_FF, NTOK + 2], BF16)
    G.upB = dscr("upB", [D_FF, NTOK], BF16)
    G.gfT = dscr("gfT", [D_FF, NTOK], BF16)
    G.dn_wT = [dscr(f"dn_wT{d}", [NT128, 128, 1024], BF16) for d in range(2)]
    G.dn_u = [dscr(f"dn_u{d}", [NT128, 128, 1024], F32) for d in range(2)]
    G.dn_aT = [dscr(f"dn_aT{d}", [NT128, 128, 1024], BF16) for d in range(2)]
    G.dn_kd = [dscr(f"dn_kd{d}", [NT128, 128, 1024], BF16) for d in range(2)]
    G.dn_qg = [dscr(f"dn_qg{d}", [NT128, 128, 1024], BF16) for d in range(2)]
    G.dn_sc = [dscr(f"dn_sc{d}", [NT128, 128, 32], F32) for d in range(2)]
    G.dn_o = [dscr(f"dn_o{d}", [NTOK, 1024], F32) for d in range(2)]
    G.db = {}

    def dbuf(name):
        if name not in G.db:
            G.db[name] = T.buf("dram_" + name)
        return G.db[name]
    G.dbuf = dbuf

    G.cols = Tl(nc.alloc_sbuf_tensor("sb_cols", [128, L * NCOLS_L], F32), T.buf("cols"))
    G.masks = Tl(nc.alloc_sbuf_tensor("sb_masks", [128, NMASK, 128], F32), T.buf("masks"))
    G.flags = Tl(nc.alloc_sbuf_tensor("sb_flags", [128, 4], F32), T.buf("flags"))
    G.identb = Tl(nc.alloc_sbuf_tensor("identb", [128, 128], BF16), T.buf("identb"))
    G.gatep = Tl(nc.alloc_sbuf_tensor("sb_gatep", [16, L * 2], F32), T.buf("gatep"))
    T.dma(T.SP, G.cols[:], G.cols_d[:, :], W=[G.cols.b], sb=G.cols.b)
    T.dma(T.SP, G.masks[:], G.masks_d[:, :, :], W=[G.masks.b], sb=G.masks.b)
    T.dma(T.SP, G.flags[:], G.flags_d[:, :], W=[G.flags.b], sb=G.flags.b)
    T.dma(T.SP, G.gatep[:], G.gatep_d[:, :], W=[G.gatep.b], sb=G.gatep.b)
    T.op(T.DVE, lambda h: h.tensor_copy(out=G.identb[:], in_=G.masks[:, M_IDENT, :]),
         R=[G.masks.b], W=[G.identb.b])
    G.PS = [nc.alloc_psum_tensor(f"ps{i}", [128, 1024], F32) for i in range(4)]
    G.PB = [T.buf(f"psb{i}") for i in range(8)]

    G.dpool = []
    G.dpool_i = 0

    def take_dbuf():
        if G.dpool_i == len(G.dpool):
            G.dpool.append(T.buf(f"dp{len(G.dpool)}"))
        b = G.dpool[G.dpool_i]
        G.dpool_i += 1
        return b
    G.take_dbuf = take_dbuf
    G.col = lambda l, c, n=1: G.cols[:, l * NCOLS_L + c: l * NCOLS_L + c + n]
    G.ones = G.masks[:, M_ONES, :]
    G.onesb_t = Tl(nc.alloc_sbuf_tensor("sb_onesb", [128, 128], BF16), T.buf("onesb"))
    T.op(T.POOL, lambda h: h.tensor_copy(out=G.onesb_t[:], in_=G.masks[:, M_ONES, :]), R=[G.masks.b], W=[G.onesb_t.b])
    G.onesb = G.onesb_t[:]
    return G


def phase_wcast(G, l):
    T = G.T
    with Phase(G, "wc") as ph:
        st32 = [ph.sb(f"s32_{i}", [128, 16 * 512], F32, dma=True) for i in range(2)]
        st16 = [ph.sb(f"s16_{i}", [128, 16 * 512], BF16, dma=True) for i in range(2)]
        cnt = [0]

        def do(W, K, tiles, Wb, wname):
            KT = K // 128
            gmax = 4 if KT == 16 else 1
            i = 0
            while i < len(tiles):
                c0, ncl = tiles[i]
                g = 1
                if ncl == 128:
                    while g < gmax and i + g < len(tiles) and tiles[i + g] == (c0 + g * 128, 128):
                        g += 1
                width = ncl if g == 1 else g * 128
                cw = width // g
                s32 = st32[cnt[0] % 2]
                s16 = st16[cnt[0] % 2]
                src = W[:, c0:c0 + width].rearrange("(kt p) c -> p kt c", p=128)
                dst32 = s32[:, 0:KT * width].rearrange("p (kt c) -> p kt c", kt=KT)
                T.dma(T.SP, dst32, src, W=[s32.b], sb=s32.b)
                in_v = s32[:, 0:KT * width].rearrange("p (kt g c) -> p g kt c", kt=KT, g=g)
                out_v = s16[:, 0:KT * width].rearrange("p (g kt c) -> p g kt c", g=g, kt=KT)
                which = cnt[0] % 2
                if which == 0:
                    T.op(T.DVE, lambda h: h.tensor_copy(out=out_v, in_=in_v), R=[s32.b], W=[s16.b])
                else:
                    T.op(T.ACT, lambda h: h.copy(out=out_v, in_=in_v), R=[s32.b], W=[s16.b])
                if ncl == 128:
                    dstd = Wb[i:i + g, :, :].rearrange("g p f -> p g f")
                    srcs = s16[:, 0:KT * width].rearrange("p (g f) -> p g f", g=g)
                else:
                    dstd = Wb[i, :, :].rearrange("p (kt c) -> p kt c", kt=KT)[:, :, 0:ncl]
                    srcs = s16[:, 0:KT * width].rearrange("p (kt c) -> p kt c", kt=KT)
                T.dma(T.SP, dstd, srcs, R=[s16.b], W=[G.dbuf(wname)], sb=s16.b)
                cnt[0] += 1
                i += g

        do(G.w_in[l], D_MODEL, win_tiles(), G.wb_in[l], f"wb_in{l}")
        do(G.w_out[l], D_MODEL, [(n * 128, 128) for n in range(16)], G.wb_out[l], f"wb_out{l}")
        do(G.w_up[l], D_MODEL, [(n * 128, 128) for n in range(88)], G.wb_up[l], f"wb_up{l}")
        do(G.w_down[l], D_FF, [(n * 128, 128) for n in range(16)], G.wb_down[l], f"wb_down{l}")


def phase_proj(G, l, kind):
    T, nc, NTOK = G.T, G.nc, G.NTOK
    PS, PB = G.PS, G.PB
    if kind == "in":
        src, srcn, K, normed, gcol, Wb, wname = G.x_cur, G.x_cur_name, D_MODEL, True, C_MIXPRE, G.wb_in[l], f"wb_in{l}"
        tiles = win_tiles()
        TB = 1024
    elif kind == "up":
        src, srcn, K, normed, gcol, Wb, wname = G.xmT, "xmT", D_MODEL, True, C_FFNPRE, G.wb_up[l], f"wb_up{l}"
        tiles = [(n * 128, 128) for n in range(88)]
        TB = 1024
    elif kind == "out":
        src, srcn, K, normed, Wb, wname = G.mixT, "mixT", D_MODEL, False, G.wb_out[l], f"wb_out{l}"
        tiles = [(n * 128, 128) for n in range(16)]
        TB = 1024
        xold, xoldn, dst, dstn, pcol = G.x_cur, G.x_cur_name, G.xmT, "xmT", C_MIXPOST
    else:
        src, srcn, K, normed, Wb, wname = G.gfT, "gfT", D_FF, False, G.wb_down[l], f"wb_down{l}"
        tiles = [(n * 128, 128) for n in range(16)]
        TB = 512
        xold, xoldn, dst, dstn, pcol = G.xmT, "xmT", G.x_next, G.x_next_name, C_FFNPOST
    TB = min(TB, NTOK)
    res = kind in ("out", "down")
    KT = K // 128
    NH = TB // 512
    NTL = len(tiles)
    NBLK = NTOK // TB
    eps_b = NORM_EPS

    with Phase(G, "pj" + kind) as ph:
        hT = ph.sb("hT", [128, KT, TB], BF16)
        GK = 4 if not normed else 1
        hb = [G.take_dbuf() if not normed else T.buf(f"hT{k}") for k in range((KT + GK - 1) // GK)]
        wr = [ph.sb(f"w{i}", [128, KT * 128], BF16, dma=True) for i in range(3)]
        rs = ph.sb("rs", [128, TB], F32)
        lnv = ph.sb("lnv", [128, TB], F32)
        sq = [ph.sb(f"sq{i}", [128, TB], BF16) for i in range(2)]
        xin = [ph.sb(f"xin{i}", [128, TB], F32, dma=True) for i in range(3)]
        if res:
            ysb = ph.sb("ysb", [128, 16, TB], F32)
            yb = [T.buf(f"y{n}") for n in range(16)]
            xo = [ph.sb(f"xo{i}", [128, TB], F32, dma=True) for i in range(2)]
        else:
            ob = [ph.sb(f"ob{i}", [128, TB], BF16, dma=True) for i in range(3)]
            og = ph.sb("og", [32, TB], F32, dma=True)
        wcnt = 0
        xcnt = 0
        ocnt = 0
        sqc = 0
        for blk in range(NBLK):
            t0 = blk * TB
            stat_bufs = [PB[4], PB[5]][:NH]
            if normed:
                for kt in range(KT):
                    xi = xin[xcnt % 3]
                    xcnt += 1
                    T.dma(T.SP, xi[:], src[kt * 128:(kt + 1) * 128, t0:t0 + TB], R=[G.dbuf(srcn)], W=[xi.b], sb=xi.b)
                    s = sq[sqc % 2]
                    sqc += 1
                    T.op(T.ACT, lambda h: h.activation(out=s[:], in_=xi[:], func=AF.Square), R=[xi.b], W=[s.b])
                    for hh in range(NH):
                        T.op(T.PE, lambda h: h.matmul(PS[2][:, hh * 512:(hh + 1) * 512], lhsT=G.onesb, rhs=s[:, hh * 512:(hh + 1) * 512],
                                                      start=(kt == 0), stop=(kt == KT - 1)),
                             R=[s.b, G.onesb_t.b], W=[stat_bufs[hh]])
                    T.op(T.DVE, lambda h: h.tensor_scalar(out=hT[:, kt, :], in0=xi[:], scalar1=G.col(l, gcol + kt), scalar2=None, op0=ALU.mult),
                         R=[xi.b, G.cols.b], W=[hb[kt]])
                T.op(T.ACT, lambda h: h.activation(out=lnv[:], in_=PS[2][:, 0:TB], func=AF.Ln, scale=1.0 / K, bias=eps_b),
                     R=stat_bufs, W=[lnv.b])
                T.op(T.ACT, lambda h: h.activation(out=rs[:], in_=lnv[:], func=AF.Exp, scale=-0.5), R=[lnv.b], W=[rs.b])
            else:
                for gi in range(len(hb)):
                    k0, k1 = gi * GK, min(KT, (gi + 1) * GK)
                    T.dma(T.SP, hT[:, k0:k1, :], src[k0 * 128:k1 * 128, t0:t0 + TB].rearrange("(kt p) t -> p kt t", p=128),
                          R=[G.dbuf(srcn)], W=[hb[gi]], sb=hb[gi])
            for nt in range(NTL):
                c0, M = tiles[nt]
                w = wr[wcnt % 3]
                wcnt += 1
                T.dma(T.SP, w[:], Wb[nt, :, :], R=[G.dbuf(wname)], W=[w.b], sb=w.b)
                pa = PS[nt % 2]
                pbs = [PB[2 * (nt % 2)], PB[2 * (nt % 2) + 1]][:NH]
                for kt in range(KT):
                    for hh in range(NH):
                        T.op(T.PE, lambda h: h.matmul(pa[0:M, hh * 512:(hh + 1) * 512], lhsT=w[:, kt * 128:kt * 128 + M],
                                                      rhs=hT[:, kt, hh * 512:(hh + 1) * 512], start=(kt == 0), stop=(kt == KT - 1)),
                             R=[w.b, hb[kt // GK]], W=[pbs[hh]], inc=(kt == KT - 1 and hh == NH - 1))
                if not res:
                    if M == 128:
                        o = ob[ocnt % 3]
                        ocnt += 1
                    else:
                        o = og
                    T.op(T.DVE, lambda h: h.tensor_tensor(out=o[0:M, :], in0=pa[0:M, 0:TB], in1=rs[0:M, :], op=ALU.mult),
                         R=pbs + [rs.b], W=[o.b])
                    if kind == "in":
                        if M == 128:
                            d_ap, dn = G.pT[nt * 128:(nt + 1) * 128, 1 + t0:1 + t0 + TB], "pT"
                        else:
                            d_ap, dn = G.graw[0:32, t0:t0 + TB], "graw"
                    else:
                        if nt < 44:
                            d_ap, dn = G.upA[nt * 128:(nt + 1) * 128, 1 + t0:1 + t0 + TB], "upA"
                        else:
                            d_ap, dn = G.upB[(nt - 44) * 128:(nt - 43) * 128, t0:t0 + TB], "upB"
                    T.dma(T.ACT, d_ap, o[0:M, :], R=[o.b], W=[G.dbuf(dn)], sb=o.b)
                else:
                    T.op(T.ACT, lambda h: h.copy(out=ysb[:, nt, :], in_=pa[:, 0:TB]), R=pbs, W=[yb[nt]])
                    s = sq[sqc % 2]
                    sqc += 1
                    T.op(T.ACT, lambda h: h.activation(out=s[:], in_=pa[:, 0:TB], func=AF.Square), R=pbs, W=[s.b])
                    for hh in range(NH):
                        T.op(T.PE, lambda h: h.matmul(PS[2][:, hh * 512:(hh + 1) * 512], lhsT=G.onesb, rhs=s[:, hh * 512:(hh + 1) * 512],
                                                      start=(nt == 0), stop=(nt == NTL - 1)),
                             R=[s.b, G.onesb_t.b], W=[stat_bufs[hh]])
            if res:
                T.op(T.ACT, lambda h: h.activation(out=lnv[:], in_=PS[2][:, 0:TB], func=AF.Ln, scale=1.0 / D_MODEL, bias=eps_b),
                     R=stat_bufs, W=[lnv.b])
                T.op(T.ACT, lambda h: h.activation(out=rs[:], in_=lnv[:], func=AF.Exp, scale=-0.5), R=[lnv.b], W=[rs.b])
                for nt in range(16):
                    xi = xin[xcnt % 3]
                    xcnt += 1
                    T.dma(T.SP, xi[:], xold[nt * 128:(nt + 1) * 128, t0:t0 + TB], R=[G.dbuf(xoldn)], W=[xi.b], sb=xi.b)
                    T.op(T.DVE, lambda h: h.tensor_tensor(out=ysb[:, nt, :], in0=ysb[:, nt, :], in1=rs[:], op=ALU.mult),
                         R=[yb[nt], rs.b], W=[yb[nt]])
                    o = xo[ocnt % 2]
                    ocnt += 1
                    T.op(T.DVE, lambda h: h.scalar_tensor_tensor(out=o[:], in0=ysb[:, nt, :], scalar=G.col(l, pcol + nt), in1=xi[:],
                                                                 op0=ALU.mult, op1=ALU.add),
                         R=[yb[nt], xi.b, G.cols.b], W=[o.b])
                    T.dma(T.ACT, dst[nt * 128:(nt + 1) * 128, t0:t0 + TB], o[:], R=[o.b], W=[G.dbuf(dstn)], sb=o.b)


def build_program(NTOK, depth=DEPTH, stop_after=None, dbg=()):
    G = build_base(NTOK, depth, dbg)
    T = G.T
    stages = []

    def stage(name, fn):
        if G.stopped:
            return
        fn()
        if stop_after == name:
            G.stopped = True
    G.stopped = False
    stage("init", lambda: phase_init(G))
    for l in range(depth):
        G.x_cur, G.x_cur_name = (G.xT, "xT") if l == 0 else (G.xsT[(l - 1) % 2], f"xsT{(l - 1) % 2}")
        G.x_next, G.x_next_name = (G.yT, "yT") if l == depth - 1 else (G.xsT[l % 2], f"xsT{l % 2}")
        stage(f"wcast{l}", lambda: phase_wcast(G, l))
        stage(f"pin{l}", lambda: phase_proj(G, l, "in"))
        if "phase_e1" in globals():
            stage(f"e1{l}", lambda: phase_e1(G, l))
            stage(f"dn1{l}", lambda: phase_dn1(G, l))
            stage(f"dn2{l}", lambda: phase_dn2(G, l))
            stage(f"dn3{l}", lambda: phase_dn3(G, l))
            stage(f"pout{l}", lambda: phase_proj(G, l, "out"))
            stage(f"pup{l}", lambda: phase_proj(G, l, "up"))
            stage(f"e2{l}", lambda: phase_e2(G, l))
            stage(f"pdown{l}", lambda: phase_proj(G, l, "down"))
    T.barrier()
    return G


def prep_small(inp, depth):
    L = depth
    cols = np.zeros((128, L * NCOLS_L), np.float32)
    for l in range(L):
        b = l * NCOLS_L
        cols[:, b + C_MIXPRE:b + C_MIXPRE + 16] = np.asarray(inp["norm_mix_pre"][l]).reshape(16, 128).T
        cols[:, b + C_MIXPOST:b + C_MIXPOST + 16] = np.asarray(inp["norm_mix_post"][l]).reshape(16, 128).T
        cols[:, b + C_FFNPRE:b + C_FFNPRE + 16] = np.asarray(inp["norm_ffn_pre"][l]).reshape(16, 128).T
        cols[:, b + C_FFNPOST:b + C_FFNPOST + 16] = np.asarray(inp["norm_ffn_post"][l]).reshape(16, 128).T
        cols[:, b + C_SCN:b + C_SCN + 8] = np.asarray(inp["sc_norm"][l]).reshape(8, 128).T
        cq = np.asarray(inp["conv_qkv"][l]).reshape(3, 24, 128)
        cols[:, b + C_CQKV:b + C_CQKV + 72] = cq.transpose(2, 1, 0).reshape(128, 72)
        cs = np.asarray(inp["conv_sc"][l]).reshape(3, 8, 128)
        cols[:, b + C_CSC:b + C_CSC + 24] = cs.transpose(2, 1, 0).reshape(128, 24)
        cf = np.asarray(inp["conv_ffn"][l]).reshape(3, 44, 128)
        cols[:, b + C_CFFN:b + C_CFFN + 132] = cf.transpose(2, 1, 0).reshape(128, 132)
    gatep = np.zeros((16, L * 2), np.float32)
    for l in range(L):
        gatep[:, 2 * l] = np.asarray(inp["a_log"][l]).reshape(16)
        gatep[:, 2 * l + 1] = np.asarray(inp["dt_bias"][l]).reshape(16)
    dnw = np.ascontiguousarray(np.asarray(inp["dn_norm"][:L], np.float32))
    return cols, gatep, dnw


def bank_ap(G, i):
    return G.PS[i // 2][:, (i % 2) * 512:(i % 2) * 512 + 512]


def halo_fix(G, tl, t0, TBE, W):
    T = G.T
    mid = G.NTOK // 2
    if t0 == mid:
        T.op(T.POOL, lambda h: h.tensor_scalar(out=tl[:, 0:1], in0=tl[:, 0:1], scalar1=G.flags[:, 0:1], scalar2=None, op0=ALU.mult),
             R=[tl.b, G.flags.b], W=[tl.b])
    if t0 + TBE == mid:
        T.op(T.POOL, lambda h: h.tensor_scalar(out=tl[:, W - 1:W], in0=tl[:, W - 1:W], scalar1=G.flags[:, 0:1], scalar2=None, op0=ALU.mult),
             R=[tl.b, G.flags.b], W=[tl.b])


def make_diags(G, ph_diag, l, cbase, ct):
    T = G.T
    for tap in range(3):
        T.op(T.DVE, lambda h: h.tensor_scalar(out=ph_diag[:, tap, :], in0=G.masks[:, M_IDENT, :], scalar1=G.col(l, cbase + ct * 3 + tap),
                                              scalar2=None, op0=ALU.mult),
             R=[G.masks.b, G.cols.b], W=[ph_diag.b])


def phase_e1(G, l):
    T, nc, NTOK = G.T, G.nc, G.NTOK
    PB = G.PB
    TBE = 512
    WD = min(NTOK // 2, 2048)
    NW = NTOK // WD
    SUB = WD // TBE
    SEG = min(NTOK, 8192)
    NSEG = NTOK // SEG
    BPS = SEG // TBE
    WPS = SEG // WD
    with Phase(G, "e1") as ph:
        diags = [ph.sb(f"diag{i}", [128, 3, 128], BF16) for i in range(2)]
        dcur = [0]
        pin = [ph.sb(f"pin{i}", [128, WD + 2], BF16, dma=True) for i in range(2)]
        pin2 = [ph.sb(f"pinb{i}", [128, WD + 2], BF16, dma=True) for i in range(2)]
        pin3 = [ph.sb(f"pinc{i}", [128, WD], BF16, dma=True) for i in range(2)]
        cx = [ph.sb(f"cx{i}", [128, WD + 2], BF16) for i in range(2)]
        S = ph.sb("S", [128, SEG], F32)
        SS = ph.sb("SS", [128, SEG], F32)
        Sb = [T.buf(f"S{i}") for i in range(BPS)]
        SSb = [T.buf(f"SS{i}") for i in range(BPS)]
        sq = [ph.sb(f"sq{i}", [128, TBE], BF16) for i in range(2)]
        ob = [ph.sb(f"ob{i}", [128, WD], BF16, dma=True) for i in range(2)]
        cnt = dict(b=0, o=0, s=0, c=0)
        rcnt = {}

        def load_w(row0, t0, ring, halo=True):
            k = id(ring)
            rcnt[k] = rcnt.get(k, 0) + 1
            tl = ring[rcnt[k] % 2]
            if halo:
                T.dma(T.SP, tl[:], G.pT[row0:row0 + 128, t0:t0 + WD + 2], R=[G.dbuf("pT")], W=[tl.b], sb=tl.b)
                halo_fix(G, tl, t0, WD, WD + 2)
            else:
                T.dma(T.SP, tl[:], G.pT[row0:row0 + 128, 1 + t0:1 + t0 + WD], R=[G.dbuf("pT")], W=[tl.b], sb=tl.b)
            return tl

        def conv(src_tl, c0):
            bi = cnt["b"] % 4
            cnt["b"] += 1
            for tap in range(3):
                T.op(T.PE, lambda h: h.matmul(bank_ap(G, bi), lhsT=diag[:, tap, :], rhs=src_tl[:, c0 + tap:c0 + tap + TBE], start=(tap == 0), stop=(tap == 2)),
                     R=[diag.b, src_tl.b], W=[PB[bi]], inc=(tap == 2))
            return bi

        def stats(src_ap, src_buf):
            s = sq[cnt["s"] % 2]
            cnt["s"] += 1
            T.op(T.ACT, lambda h: h.activation(out=s[:], in_=src_ap, func=AF.Square), R=[src_buf], W=[s.b])
            bi = 4 + (cnt["s"] % 2)
            T.op(T.PE, lambda h: h.matmul(bank_ap(G, bi), lhsT=G.onesb, rhs=s[:], start=True, stop=True), R=[s.b, G.onesb_t.b], W=[PB[bi]])
            return bi

        def store(o, dst_ap, dname):
            T.dma(T.ACT, dst_ap, o[:], R=[o.b], W=[G.dbuf(dname)], sb=o.b)

        def next_ob():
            o = ob[cnt["o"] % 2]
            cnt["o"] += 1
            return o

        for h_ in range(8):
            for wi in range(NW):
                t0 = wi * WD
                tl = load_w(3072 + h_ * 128, t0, pin3, halo=False)
                o = next_ob()
                T.op(T.ACT, lambda h: h.activation(out=o[:], in_=tl[:], func=AF.Silu), R=[tl.b], W=[o.b])
                store(o, G.szT[h_ * 128:(h_ + 1) * 128, t0:t0 + WD], "szT")
        for h_ in range(8):
            dcur[0] += 1
            diag = diags[dcur[0] % 2]
            make_diags(G, diag, l, C_CQKV, 16 + h_)
            for wi in range(NW):
                t0 = wi * WD
                tl = load_w(2048 + h_ * 128, t0, pin)
                o = next_ob()
                for j in range(SUB):
                    c0 = j * TBE
                    bi = conv(tl, c0)
                    T.op(T.ACT, lambda h: h.activation(out=o[:, c0:c0 + TBE], in_=bank_ap(G, bi), func=AF.Silu), R=[PB[bi]], W=[o.b])
                store(o, G.qkvT[2048 + h_ * 128:2048 + (h_ + 1) * 128, t0:t0 + WD], "qkvT")
        for which in range(2):
            for h_ in range(8):
                dcur[0] += 1
                diag = diags[dcur[0] % 2]
                make_diags(G, diag, l, C_CQKV, which * 8 + h_)
                r0 = which * 1024 + h_ * 128
                for seg in range(NSEG):
                    for w_ in range(WPS):
                        t0 = seg * SEG + w_ * WD
                        tl = load_w(r0, t0, pin)
                        for j in range(SUB):
                            c0 = j * TBE
                            b_ = w_ * SUB + j
                            bi = conv(tl, c0)
                            sl = S[:, b_ * TBE:(b_ + 1) * TBE]
                            T.op(T.ACT, lambda h: h.activation(out=sl, in_=bank_ap(G, bi), func=AF.Silu), R=[PB[bi]], W=[Sb[b_]])
                            si = stats(sl, Sb[b_])
                            T.op(T.DVE, lambda h: h.tensor_scalar(out=SS[:, b_ * TBE:(b_ + 1) * TBE], in0=bank_ap(G, si), scalar1=L2_EPS, scalar2=None, op0=ALU.add),
                                 R=[PB[si]], W=[SSb[b_]])
                    for b_ in range(BPS):
                        ssl = SS[:, b_ * TBE:(b_ + 1) * TBE]
                        T.op(T.ACT, lambda h: h.activation(out=ssl, in_=ssl, func=AF.Ln), R=[SSb[b_]], W=[SSb[b_]])
                        T.op(T.ACT, lambda h: h.activation(out=ssl, in_=ssl, func=AF.Exp, scale=-0.5), R=[SSb[b_]], W=[SSb[b_]])
                    for w_ in range(WPS):
                        t0 = seg * SEG + w_ * WD
                        o = next_ob()
                        for j in range(SUB):
                            c0 = j * TBE
                            b_ = w_ * SUB + j
                            T.op(T.DVE, lambda h: h.scalar_tensor_tensor(out=o[:, c0:c0 + TBE], in0=S[:, b_ * TBE:(b_ + 1) * TBE], scalar=(HD ** -0.5 if which == 0 else 1.0),
                                                                         in1=SS[:, b_ * TBE:(b_ + 1) * TBE], op0=ALU.mult, op1=ALU.mult),
                                 R=[Sb[b_], SSb[b_]], W=[o.b])
                        store(o, G.qkvT[r0:r0 + 128, t0:t0 + WD], "qkvT")
        for g_ in range(8):
            dcur[0] += 1
            diag = diags[dcur[0] % 2]
            make_diags(G, diag, l, C_CSC, g_)
            for seg in range(NSEG):
                for w_ in range(WPS):
                    t0 = seg * SEG + w_ * WD
                    bt = load_w(4096 + g_ * 128, t0, pin3, halo=False)
                    ct_ = load_w(5120 + g_ * 128, t0, pin)
                    xt_ = load_w(6144 + g_ * 128, t0, pin2)
                    c = cx[cnt["c"] % 2]
                    cnt["c"] += 1
                    T.op(T.POOL, lambda h: h.tensor_tensor(out=c[:], in0=ct_[:], in1=xt_[:], op=ALU.mult), R=[ct_.b, xt_.b], W=[c.b])
                    for j in range(SUB):
                        c0 = j * TBE
                        b_ = w_ * SUB + j
                        bi = conv(c, c0)
                        sl = S[:, b_ * TBE:(b_ + 1) * TBE]
                        T.op(T.DVE, lambda h: h.tensor_tensor(out=sl, in0=bank_ap(G, bi), in1=bt[:, c0:c0 + TBE], op=ALU.mult), R=[PB[bi], bt.b], W=[Sb[b_]])
                        si = stats(sl, Sb[b_])
                        T.op(T.DVE, lambda h: h.tensor_scalar(out=SS[:, b_ * TBE:(b_ + 1) * TBE], in0=bank_ap(G, si), scalar1=1.0 / 128, scalar2=NORM_EPS,
                                                              op0=ALU.mult, op1=ALU.add),
                             R=[PB[si]], W=[SSb[b_]])
                for b_ in range(BPS):
                    ssl = SS[:, b_ * TBE:(b_ + 1) * TBE]
                    T.op(T.ACT, lambda h: h.activation(out=ssl, in_=ssl, func=AF.Ln), R=[SSb[b_]], W=[SSb[b_]])
                    T.op(T.ACT, lambda h: h.activation(out=ssl, in_=ssl, func=AF.Exp, scale=-0.5), R=[SSb[b_]], W=[SSb[b_]])
                for w_ in range(WPS):
                    t0 = seg * SEG + w_ * WD
                    o = next_ob()
                    for j in range(SUB):
                        c0 = j * TBE
                        b_ = w_ * SUB + j
                        T.op(T.DVE, lambda h: h.scalar_tensor_tensor(out=o[:, c0:c0 + TBE], in0=S[:, b_ * TBE:(b_ + 1) * TBE], scalar=G.col(l, C_SCN + g_),
                                                                     in1=SS[:, b_ * TBE:(b_ + 1) * TBE], op0=ALU.mult, op1=ALU.mult),
                             R=[Sb[b_], SSb[b_], G.cols.b], W=[o.b])
                    store(o, G.mixT[1024 + g_ * 128:1024 + (g_ + 1) * 128, t0:t0 + WD], "mixT")
        TG = min(NTOK, 512)
        ga = [ph.sb(f"ga{i}", [16, TG], F32, dma=True) for i in range(2)]
        gb = [ph.sb(f"gb{i}", [16, TG], F32, dma=True) for i in range(2)]
        go = [ph.sb(f"go{i}", [16, TG], F32, dma=True) for i in range(2)]
        go2 = [ph.sb(f"goo{i}", [16, TG], F32, dma=True) for i in range(2)]
        nA = ph.sb("nA", [16, 1], F32)
        T.op(T.ACT, lambda h: h.activation(out=nA[:], in_=G.gatep[:, 2 * l:2 * l + 1], func=AF.Exp), R=[G.gatep.b], W=[nA.b])
        T.op(T.DVE, lambda h: h.tensor_scalar(out=nA[:], in0=nA[:], scalar1=-1.0, scalar2=None, op0=ALU.mult), R=[nA.b], W=[nA.b])
        for i in range(NTOK // TG):
            t0 = i * TG
            a_t, b_t, o1, o2 = ga[i % 2], gb[i % 2], go[i % 2], go2[i % 2]
            T.dma(T.SP, a_t[:], G.graw[0:16, t0:t0 + TG], R=[G.dbuf("graw")], W=[a_t.b], sb=a_t.b)
            T.dma(T.SP, b_t[:], G.graw[16:32, t0:t0 + TG], R=[G.dbuf("graw")], W=[b_t.b], sb=b_t.b)
            T.op(T.ACT, lambda h: h.activation(out=a_t[:], in_=a_t[:], func=AF.Exp, bias=G.gatep[:, 2 * l + 1:2 * l + 2]), R=[a_t.b, G.gatep.b], W=[a_t.b])
            T.op(T.ACT, lambda h: h.activation(out=a_t[:], in_=a_t[:], func=AF.Ln, bias=1.0), R=[a_t.b], W=[a_t.b])
            T.op(T.DVE, lambda h: h.tensor_scalar(out=o1[:], in0=a_t[:], scalar1=nA[:, 0:1], scalar2=None, op0=ALU.mult), R=[a_t.b, nA.b], W=[o1.b])
            T.dma(T.ACT, G.gbT[0:16, t0:t0 + TG], o1[:], R=[o1.b], W=[G.dbuf("gbT")], sb=o1.b)
            T.op(T.ACT, lambda h: h.activation(out=b_t[:], in_=b_t[:], func=AF.Exp, scale=-1.0), R=[b_t.b], W=[b_t.b])
            T.op(T.DVE, lambda h: h.tensor_scalar(out=b_t[:], in0=b_t[:], scalar1=1.0, scalar2=None, op0=ALU.add), R=[b_t.b], W=[b_t.b])
            T.op(T.DVE, lambda h: h.reciprocal(out=o2[:], in_=b_t[:]), R=[b_t.b], W=[o2.b])
            T.dma(T.ACT, G.gbT[16:32, t0:t0 + TG], o2[:], R=[o2.b], W=[G.dbuf("gbT")], sb=o2.b)


def bc_h(ap2d):
    return ap2d.unsqueeze(1).to_broadcast([128, 8, 128])


def bc_x(ap_h):
    return ap_h.unsqueeze(2).to_broadcast([128, 8, 128])


def v3(ap):
    return ap.rearrange("p (h x) -> p h x", h=8)


def phase_dn1(G, l):
    T, nc, NTOK = G.T, G.nc, G.NTOK
    PS, PB = G.PS, G.PB
    ST = 512 if NTOK >= 512 else NTOK
    NST = NTOK // ST
    identf = G.masks[:, M_IDENT, :]
    with Phase(G, "dn1") as ph:
        qS = [ph.sb(f"qS{i}", [128, 8, ST], BF16, dma=True) for i in range(2)]
        kS = [ph.sb(f"kS{i}", [128, 8, ST], BF16, dma=True) for i in range(2)]
        vS = [ph.sb(f"vS{i}", [128, 8, ST], BF16, dma=True) for i in range(2)]
        gS = [ph.sb(f"gS{i}", [32, ST], F32, dma=True) for i in range(2)]
        ktm = ph.sb("ktm", [128, 1024], BF16)
        vtm = ph.sb("vtm", [128, 1024], BF16)
        qtm = ph.sb("qtm", [128, 1024], BF16)
        gt = ph.sb("gt", [128, 32], F32)
        QKs = ph.sb("QKs", [128, 1024], F32)
        KKs = ph.sb("KKs", [128, 1024], F32)

        class D:
            pass
        dd = []
        for d in range(2):
            o = D()
            o.ex = ph.sb(f"ex{d}", [128, 32], F32, dma=True)
            o.bg = ph.sb(f"bg{d}", [128, 8], F32)
            o.Gm = ph.sb(f"Gm{d}", [128, 1024], F32)
            o.E = ph.sb(f"E{d}", [128, 1024], F32)
            o.EL = ph.sb(f"EL{d}", [128, 1024], F32)
            o.aqk = ph.sb(f"aqk{d}", [128, 1024], BF16)
            o.aT = ph.sb(f"aT{d}", [128, 1024], BF16, dma=True)
            o.X = [ph.sb(f"X{d}_{i}", [128, 1024], BF16) for i in range(2)]
            o.XT = [ph.sb(f"XT{d}_{i}", [128, 1024], BF16) for i in range(2)]
            o.RT = [ph.sb(f"RT{d}_{i}", [128, 1024], BF16) for i in range(2)]
            o.kbg = ph.sb(f"kbg{d}", [128, 1024], BF16)
            o.vb = ph.sb(f"vb{d}", [128, 1024], BF16)
            o.kd = ph.sb(f"kd{d}", [128, 1024], BF16, dma=True)
            o.qgm = ph.sb(f"qgm{d}", [128, 1024], BF16)
            o.qgT = ph.sb(f"qgT{d}", [128, 1024], BF16, dma=True)
            o.wT = ph.sb(f"wT{d}", [128, 1024], BF16, dma=True)
            o.u = ph.sb(f"u{d}", [128, 1024], F32, dma=True)
            dd.append(o)
        pcnt = [0]

        def ps2():
            i = pcnt[0] % 4
            pcnt[0] += 1
            return PS[i], [PB[2 * i], PB[2 * i + 1]]

        def load_st(st_):
            s0 = st_ * ST
            sl = st_ % 2
            for (ring, which) in ((qS, 0), (kS, 1), (vS, 2)):
                T.dma(T.SP, ring[sl][:], G.qkvT[which * 1024:(which + 1) * 1024, s0:s0 + ST].rearrange("(h d) t -> d h t", d=128),
                      R=[G.dbuf("qkvT")], W=[ring[sl].b], sb=ring[sl].b)
            T.dma(T.SP, gS[sl][:], G.gbT[:, s0:s0 + ST], R=[G.dbuf("gbT")], W=[gS[sl].b], sb=gS[sl].b)

        load_st(0)
        for st_ in range(NST):
            s0 = st_ * ST
            sl = st_ % 2
            if st_ + 1 < NST:
                load_st(st_ + 1)
            q_, k_, v_, g_ = qS[sl], kS[sl], vS[sl], gS[sl]
            for j in range(ST // 128):
                c0 = j * 128
                tile = (s0 + c0) // 128
                for (src, dst) in ((k_, ktm), (v_, vtm), (q_, qtm)):
                    p, pb = ps2()
                    pbf = p[:, :].bitcast(BF16)
                    for h_ in range(8):
                        T.op(T.PE, lambda h: h.transpose(pbf[:, h_ * 128:(h_ + 1) * 128], src[:, h_, c0:c0 + 128], G.identb[:]),
                             R=[src.b, G.identb.b], W=[pb[0]], inc=(h_ == 7))
                    T.op(T.ACT, lambda h: h.copy(out=dst[:], in_=pbf[:, 0:1024]), R=[pb[0]], W=[dst.b])
                p, pb = ps2()
                T.op(T.PE, lambda h: h.transpose(p[:, 0:32], g_[:, c0:c0 + 128], identf[0:32, 0:32]), R=[g_.b, G.masks.b], W=[pb[0]])
                T.op(T.DVE, lambda h: h.tensor_copy(out=gt[:], in_=p[:, 0:32]), R=[pb[0]], W=[gt.b])
                for (lh, dst) in ((q_, QKs), (k_, KKs)):
                    p, pb = ps2()
                    for h_ in range(8):
                        T.op(T.PE, lambda h: h.matmul(p[:, h_ * 128:(h_ + 1) * 128], lhsT=lh[:, h_, c0:c0 + 128], rhs=k_[:, h_, c0:c0 + 128], start=True, stop=True),
                             R=[lh.b, k_.b], W=[pb[h_ // 4]], inc=(h_ == 7))
                    T.op(T.ACT, lambda h: h.copy(out=dst[:], in_=p[:, :]), R=pb, W=[dst.b])
                stg = [None, None]
                for d in range(2):
                    o = dd[d]
                    gd = gt[:, d * 8:(d + 1) * 8]
                    bd = gt[:, 16 + d * 8:16 + (d + 1) * 8]
                    p, pb = ps2()
                    for ci, mk in enumerate((M_U[d], M_BU[d], M_SEL0, M_SEL1)):
                        T.op(T.PE, lambda h: h.matmul(p[:, ci * 8:(ci + 1) * 8], lhsT=G.masks[:, mk, :], rhs=gd, start=True, stop=True),
                             R=[G.masks.b, gt.b], W=[pb[0]], inc=(ci == 3))
                    T.op(T.ACT, lambda h: h.activation(out=o.ex[:], in_=p[:, 0:32], func=AF.Exp), R=[pb[0]], W=[o.ex.b])
                    T.dma(T.ACT, G.dn_sc[d][tile, :, :], o.ex[:], R=[o.ex.b], W=[G.dbuf(f"dn_sc{d}")], sb=o.ex.b)
                    T.op(T.DVE, lambda h: h.tensor_tensor(out=o.bg[:], in0=bd, in1=o.ex[:, 0:8], op=ALU.mult), R=[gt.b, o.ex.b], W=[o.bg.b])
                    T.op(T.POOL, lambda h: h.tensor_tensor(out=v3(o.Gm[:, :]), in0=bc_h(G.masks[:, M_U[d], :]), in1=bc_x(gd), op=ALU.mult),
                         R=[G.masks.b, gt.b], W=[o.Gm.b])
                    p, pb = ps2()
                    for h_ in range(8):
                        T.op(T.PE, lambda h: h.matmul(p[:, h_ * 128:(h_ + 1) * 128], lhsT=o.Gm[:, h_ * 128:(h_ + 1) * 128], rhs=G.masks[:, M_LC[d], :], start=True, stop=True),
                             R=[o.Gm.b, G.masks.b], W=[pb[h_ // 4]], inc=(h_ == 7))
                    T.op(T.ACT, lambda h: h.activation(out=o.E[:], in_=p[:, :], func=AF.Exp), R=pb, W=[o.E.b])
                    T.op(T.POOL, lambda h: h.tensor_tensor(out=v3(o.EL[:, :]), in0=v3(o.E[:, :]), in1=bc_h(G.masks[:, M_INCL[d], :]), op=ALU.mult),
                         R=[o.E.b, G.masks.b], W=[o.EL.b])
                    T.op(T.POOL, lambda h: h.tensor_tensor(out=v3(o.E[:, :]), in0=v3(o.E[:, :]), in1=bc_h(G.masks[:, M_NSTRICT[d], :]), op=ALU.mult),
                         R=[o.E.b, G.masks.b], W=[o.E.b])
                    T.op(T.POOL, lambda h: h.tensor_tensor(out=v3(o.E[:, :]), in0=v3(o.E[:, :]), in1=bc_x(bd), op=ALU.mult),
                         R=[o.E.b, gt.b], W=[o.E.b])
                    T.op(T.DVE, lambda h: h.tensor_tensor(out=o.aqk[:], in0=QKs[:], in1=o.EL[:], op=ALU.mult), R=[QKs.b, o.EL.b], W=[o.aqk.b])
                    T.op(T.DVE, lambda h: h.tensor_tensor(out=o.X[0][:], in0=KKs[:], in1=o.E[:], op=ALU.mult), R=[KKs.b, o.E.b], W=[o.X[0].b])
                    T.op(T.POOL, lambda h: h.tensor_tensor(out=v3(o.kbg[:, :]), in0=v3(ktm[:, :]), in1=bc_x(o.bg[:, :]), op=ALU.mult),
                         R=[ktm.b, o.bg.b], W=[o.kbg.b])
                    T.op(T.POOL, lambda h: h.tensor_tensor(out=v3(o.vb[:, :]), in0=v3(vtm[:, :]), in1=bc_x(bd), op=ALU.mult),
                         R=[vtm.b, gt.b], W=[o.vb.b])
                    T.op(T.POOL, lambda h: h.tensor_tensor(out=v3(o.kd[:, :]), in0=v3(ktm[:, :]), in1=bc_x(o.ex[:, 8:16]), op=ALU.mult),
                         R=[ktm.b, o.ex.b], W=[o.kd.b])
                    T.dma(T.SP, G.dn_kd[d][tile, :, :], o.kd[:], R=[o.kd.b], W=[G.dbuf(f"dn_kd{d}")], sb=o.kd.b)
                    T.op(T.POOL, lambda h: h.tensor_tensor(out=v3(o.qgm[:, :]), in0=v3(qtm[:, :]), in1=bc_x(o.ex[:, 0:8]), op=ALU.mult),
                         R=[qtm.b, o.ex.b], W=[o.qgm.b])
                for d in range(2):
                    o = dd[d]
                    p, pb = ps2()
                    pbf = p[:, :].bitcast(BF16)
                    for h_ in range(8):
                        T.op(T.PE, lambda h: h.transpose(pbf[:, h_ * 128:(h_ + 1) * 128], o.aqk[:, h_ * 128:(h_ + 1) * 128], G.identb[:]),
                             R=[o.aqk.b, G.identb.b], W=[pb[0]], inc=(h_ == 7))
                    for h_ in range(8):
                        T.op(T.PE, lambda h: h.transpose(pbf[:, 1024 + h_ * 128:1024 + (h_ + 1) * 128], o.X[0][:, h_ * 128:(h_ + 1) * 128], G.identb[:]),
                             R=[o.X[0].b, G.identb.b], W=[pb[1]], inc=(h_ == 7))
                    T.op(T.ACT, lambda h: h.copy(out=o.aT[:], in_=pbf[:, 0:1024]), R=[pb[0]], W=[o.aT.b])
                    T.dma(T.SP, G.dn_aT[d][tile, :, :], o.aT[:], R=[o.aT.b], W=[G.dbuf(f"dn_aT{d}")], sb=o.aT.b)
                    T.op(T.DVE, lambda h: h.tensor_copy(out=o.XT[0][:], in_=pbf[:, 1024:2048]), R=[pb[1]], W=[o.XT[0].b])
                    T.op(T.POOL, lambda h: h.tensor_tensor(out=v3(o.RT[0][:, :]), in0=v3(o.XT[0][:, :]), in1=bc_h(G.identb[:]), op=ALU.add),
                         R=[o.XT[0].b, G.identb.b], W=[o.RT[0].b])
                    p, pb = ps2()
                    pbf = p[:, :].bitcast(BF16)
                    for h_ in range(8):
                        T.op(T.PE, lambda h: h.transpose(pbf[:, h_ * 128:(h_ + 1) * 128], o.qgm[:, h_ * 128:(h_ + 1) * 128], G.identb[:]),
                             R=[o.qgm.b, G.identb.b], W=[pb[0]], inc=(h_ == 7))
                    T.op(T.ACT, lambda h: h.copy(out=o.qgT[:], in_=pbf[:, 0:1024]), R=[pb[0]], W=[o.qgT.b])
                    T.dma(T.SP, G.dn_qg[d][tile, :, :], o.qgT[:], R=[o.qgT.b], W=[G.dbuf(f"dn_qg{d}")], sb=o.qgT.b)
                for k in range(1, 6):
                    a, b = (k - 1) % 2, k % 2
                    for d in range(2):
                        o = dd[d]
                        p, pb = ps2()
                        for h_ in range(8):
                            hs = slice(h_ * 128, (h_ + 1) * 128)
                            T.op(T.PE, lambda h: h.matmul(p[:, hs], lhsT=o.XT[a][:, hs], rhs=o.X[a][:, hs], start=True, stop=True),
                                 R=[o.XT[a].b, o.X[a].b], W=[pb[h_ // 4]], inc=(h_ == 7))
                        T.op(T.ACT, lambda h: h.copy(out=o.X[b][:], in_=p[:, :]), R=pb, W=[o.X[b].b])
                        if k < 5:
                            p, pb = ps2()
                            for h_ in range(8):
                                hs = slice(h_ * 128, (h_ + 1) * 128)
                                T.op(T.PE, lambda h: h.matmul(p[:, hs], lhsT=o.X[a][:, hs], rhs=o.XT[a][:, hs], start=True, stop=True),
                                     R=[o.XT[a].b, o.X[a].b], W=[pb[h_ // 4]], inc=(h_ == 7))
                            T.op(T.DVE, lambda h: h.tensor_copy(out=o.XT[b][:], in_=p[:, :]), R=pb, W=[o.XT[b].b])
                    for d in range(2):
                        o = dd[d]
                        p, pb = ps2()
                        for h_ in range(8):
                            hs = slice(h_ * 128, (h_ + 1) * 128)
                            T.op(T.PE, lambda h: h.matmul(p[:, hs], lhsT=o.X[b][:, hs], rhs=o.RT[a][:, hs], start=True, stop=True),
                                 R=[o.X[b].b, o.RT[a].b], W=[pb[h_ // 4]], inc=(h_ == 7))
                        T.op(T.DVE, lambda h: h.tensor_tensor(out=o.RT[b][:], in0=p[:, :], in1=o.RT[a][:], op=ALU.add), R=pb + [o.RT[a].b], W=[o.RT[b].b])
                for d in range(2):
                    o = dd[d]
                    TT = o.RT[1]
                    p, pb = ps2()
                    for h_ in range(8):
                        hs = slice(h_ * 128, (h_ + 1) * 128)
                        T.op(T.PE, lambda h: h.matmul(p[:, hs], lhsT=o.kbg[:, hs], rhs=TT[:, hs], start=True, stop=True),
                             R=[o.kbg.b, TT.b], W=[pb[h_ // 4]], inc=(h_ == 7))
                    T.op(T.ACT, lambda h: h.copy(out=o.wT[:], in_=p[:, :]), R=pb, W=[o.wT.b])
                    T.dma(T.SP, G.dn_wT[d][tile, :, :], o.wT[:], R=[o.wT.b], W=[G.dbuf(f"dn_wT{d}")], sb=o.wT.b)
                    p, pb = ps2()
                    for h_ in range(8):
                        hs = slice(h_ * 128, (h_ + 1) * 128)
                        T.op(T.PE, lambda h: h.matmul(p[:, hs], lhsT=TT[:, hs], rhs=o.vb[:, hs], start=True, stop=True),
                             R=[o.vb.b, TT.b], W=[pb[h_ // 4]], inc=(h_ == 7))
                    T.op(T.DVE, lambda h: h.tensor_copy(out=o.u[:], in_=p[:, :]), R=pb, W=[o.u.b])
                    T.dma(T.SP, G.dn_u[d][tile, :, :], o.u[:], R=[o.u.b], W=[G.dbuf(f"dn_u{d}")], sb=o.u.b)


def phase_dn2(G, l):
    T, nc, NTOK = G.T, G.nc, G.NTOK
    PS, PB = G.PS, G.PB
    NTL = NTOK // 128
    NCHK = NTOK // CH
    with Phase(G, "dn2") as ph:
        class D:
            pass
        dd = []
        for d in range(2):
            o = D()
            o.wT = [ph.sb(f"wT{d}_{i}", [128, 1024], BF16, dma=True) for i in range(2)]
            o.qg = [ph.sb(f"qg{d}_{i}", [128, 1024], BF16, dma=True) for i in range(2)]
            o.u2 = [ph.sb(f"u2{d}_{i}", [64, 2, 1024], F32, dma=True) for i in range(2)]
            o.a2 = [ph.sb(f"a2{d}_{i}", [64, 2, 1024], BF16, dma=True) for i in range(2)]
            o.k2 = [ph.sb(f"k2{d}_{i}", [64, 2, 1024], BF16, dma=True) for i in range(2)]
            o.gl = [ph.sb(f"gl{d}_{i}", [128, 16], F32, dma=True) for i in range(2)]
            o.o2 = [ph.sb(f"o2{d}_{i}", [128, 2, 1024], F32, dma=True) for i in range(2)]
            o.S = ph.sb(f"S{d}", [128, 1024], F32)
            o.Sb = ph.sb(f"Sb{d}", [128, 1024], BF16)
            o.Sd = ph.sb(f"Sd{d}", [128, 1024], F32)
            o.vn = ph.sb(f"vn{d}", [64, 1024], BF16)
            o.pa, o.pab = PS[2 * d], [PB[4 * d], PB[4 * d + 1]]
            o.pk, o.pkb = PS[2 * d + 1], [PB[4 * d + 2], PB[4 * d + 3]]
            T.op(T.DVE, lambda h: h.memset(o.S[:], 0.0), W=[o.S.b])
            T.op(T.DVE, lambda h: h.memset(o.Sb[:], 0.0), W=[o.Sb.b])
            dd.append(o)

        def tile_of(d, n):
            return n if d == 0 else NTL - 1 - n

        def load_tile(d, n):
            o = dd[d]
            t = tile_of(d, n)
            s = n % 2
            T.dma(T.SP, o.wT[s][:], G.dn_wT[d][t, :, :], R=[G.dbuf(f"dn_wT{d}")], W=[o.wT[s].b], sb=o.wT[s].b)
            T.dma(T.SP, o.qg[s][:], G.dn_qg[d][t, :, :], R=[G.dbuf(f"dn_qg{d}")], W=[o.qg[s].b], sb=o.qg[s].b)
            T.dma(T.SP, o.u2[s][:], G.dn_u[d][t, :, :].rearrange("(c p) f -> p c f", p=64), R=[G.dbuf(f"dn_u{d}")], W=[o.u2[s].b], sb=o.u2[s].b)
            T.dma(T.SP, o.a2[s][:], G.dn_aT[d][t, :, :].rearrange("(c p) f -> p c f", p=64), R=[G.dbuf(f"dn_aT{d}")], W=[o.a2[s].b], sb=o.a2[s].b)
            T.dma(T.SP, o.k2[s][:], G.dn_kd[d][t, :, :].rearrange("(c p) f -> p c f", p=64), R=[G.dbuf(f"dn_kd{d}")], W=[o.k2[s].b], sb=o.k2[s].b)
            T.dma(T.SP, o.gl[s][:], G.dn_sc[d][t, :, 16:32], R=[G.dbuf(f"dn_sc{d}")], W=[o.gl[s].b], sb=o.gl[s].b)

        for d in range(2):
            load_tile(d, 0)
        for n in range(NTL):
            for d in range(2):
                if n + 1 < NTL:
                    load_tile(d, n + 1)
            for cc in range(2):
                for d in range(2):
                    o = dd[d]
                    s = n % 2
                    c = cc if d == 0 else 1 - cc
                    t = tile_of(d, n)
                    chunk = t * 2 + c
                    cs = slice(c * 64, c * 64 + 64)
                    if (d == 0 and chunk == NCHK // 2) or (d == 1 and chunk == NCHK // 2 - 1):
                        T.op(T.DVE, lambda h: h.tensor_scalar(out=o.S[:], in0=o.S[:], scalar1=G.flags[:, 0:1], scalar2=None, op0=ALU.mult),
                             R=[o.S.b, G.flags.b], W=[o.S.b])
                        T.op(T.ACT, lambda h: h.copy(out=o.Sb[:], in_=o.S[:]), R=[o.S.b], W=[o.Sb.b])
                    wT3, qg3 = v3(o.wT[s][:, :]), v3(o.qg[s][:, :])
                    for h_ in range(8):
                        hs = slice(h_ * 128, (h_ + 1) * 128)
                        T.op(T.PE, lambda h: h.matmul(o.pa[0:64, hs], lhsT=wT3[:, h_, cs], rhs=o.Sb[:, hs], start=True, stop=True),
                             R=[o.wT[s].b, o.Sb.b], W=[o.pab[h_ // 4]], inc=(h_ == 7))
                    T.op(T.DVE, lambda h: h.tensor_tensor(out=o.vn[:], in0=o.u2[s][:, c, :], in1=o.pa[0:64, :], op=ALU.subtract),
                         R=[o.u2[s].b] + o.pab, W=[o.vn.b])
                    T.op(T.POOL, lambda h: h.tensor_tensor(out=v3(o.Sd[:, :]), in0=v3(o.S[:, :]), in1=bc_x(o.gl[s][:, c * 8:(c + 1) * 8]), op=ALU.mult),
                         R=[o.S.b, o.gl[s].b], W=[o.Sd.b])
                    for h_ in range(8):
                        hs = slice(h_ * 128, (h_ + 1) * 128)
                        T.op(T.PE, lambda h: h.matmul(o.pk[:, hs], lhsT=o.k2[s][:, c, hs], rhs=o.vn[:, hs], start=True, stop=True),
                             R=[o.k2[s].b, o.vn.b], W=[o.pkb[h_ // 4]], inc=(h_ == 7))
                    for h_ in range(8):
                        hs = slice(h_ * 128, (h_ + 1) * 128)
                        T.op(T.PE, lambda h: h.matmul(o.pa[64:128, hs], lhsT=qg3[:, h_, cs], rhs=o.Sb[:, hs], start=True, stop=False),
                             R=[o.qg[s].b, o.Sb.b], W=[o.pab[h_ // 4]], inc=False)
                        T.op(T.PE, lambda h: h.matmul(o.pa[64:128, hs], lhsT=o.a2[s][:, c, h_ * 128 + c * 64:h_ * 128 + c * 64 + 64], rhs=o.vn[:, hs],
                                                      start=False, stop=True),
                             R=[o.a2[s].b, o.vn.b], W=[o.pab[h_ // 4]], inc=(h_ == 7))
                    T.op(T.DVE, lambda h: h.tensor_tensor(out=o.S[:], in0=o.Sd[:], in1=o.pk[:, :], op=ALU.add), R=[o.Sd.b] + o.pkb, W=[o.S.b])
                    T.op(T.ACT, lambda h: h.copy(out=o.Sb[:], in_=o.S[:]), R=[o.S.b], W=[o.Sb.b])
                    T.op(T.ACT, lambda h: h.copy(out=o.o2[s][64:128, c, :], in_=o.pa[64:128, :]), R=o.pab, W=[o.o2[s].b])
            for d in range(2):
                o = dd[d]
                s = n % 2
                t = tile_of(d, n)
                T.dma(T.ACT, G.dn_o[d][t * 128:(t + 1) * 128, :].rearrange("(c p) f -> p c f", p=64), o.o2[s][64:128, :, :],
                      R=[o.o2[s].b], W=[G.dbuf(f"dn_o{d}")], sb=o.o2[s].b)


def phase_dn3(G, l):
    T, nc, NTOK = G.T, G.nc, G.NTOK
    PS, PB = G.PS, G.PB
    NTL = NTOK // 128
    with Phase(G, "dn3") as ph:
        of = [ph.sb(f"of{i}", [128, 1024], F32, dma=True) for i in range(2)]
        obw = [ph.sb(f"obw{i}", [128, 1024], F32, dma=True) for i in range(2)]
        sz = [ph.sb(f"sz{i}", [128, 8, 128], BF16, dma=True) for i in range(2)]
        osum = ph.sb("osum", [128, 1024], F32)
        sqt = ph.sb("sqt", [128, 1024], F32)
        ss = ph.sb("ss", [128, 8], F32)
        on = ph.sb("on", [128, 1024], BF16)
        dnw = ph.sb("dnw", [128, 128], F32, dma=True)
        oo = [ph.sb(f"oo{i}", [128, 8, 128], BF16, dma=True) for i in range(2)]
        T.dma(T.SP, dnw[:], G.dnw_d[l, :].partition_broadcast(128), W=[dnw.b], sb=dnw.b)
        for t in range(NTL):
            s = t % 2
            T.dma(T.SP, of[s][:], G.dn_o[0][t * 128:(t + 1) * 128, :], R=[G.dbuf("dn_o0")], W=[of[s].b], sb=of[s].b)
            T.dma(T.SP, obw[s][:], G.dn_o[1][t * 128:(t + 1) * 128, :], R=[G.dbuf("dn_o1")], W=[obw[s].b], sb=obw[s].b)
            T.dma(T.SP, sz[s][:], G.szT[:, t * 128:(t + 1) * 128].rearrange("(h e) t -> e h t", e=128), R=[G.dbuf("szT")], W=[sz[s].b], sb=sz[s].b)
            T.op(T.POOL, lambda h: h.tensor_tensor(out=osum[:], in0=of[s][:], in1=obw[s][:], op=ALU.add), R=[of[s].b, obw[s].b], W=[osum.b])
            T.op(T.ACT, lambda h: h.activation(out=sqt[:], in_=osum[:], func=AF.Square), R=[osum.b], W=[sqt.b])
            T.op(T.DVE, lambda h: h.tensor_reduce(out=ss[:], in_=v3(sqt[:, :]), axis=AX.X, op=ALU.add), R=[sqt.b], W=[ss.b])
            T.op(T.ACT, lambda h: h.activation(out=ss[:], in_=ss[:], func=AF.Ln, scale=1.0 / 128, bias=NORM_EPS), R=[ss.b], W=[ss.b])
            T.op(T.ACT, lambda h: h.activation(out=ss[:], in_=ss[:], func=AF.Exp, scale=-0.5), R=[ss.b], W=[ss.b])
            T.op(T.DVE, lambda h: h.tensor_tensor(out=v3(osum[:, :]), in0=v3(osum[:, :]), in1=bc_x(ss[:, :]), op=ALU.mult), R=[osum.b, ss.b], W=[osum.b])
            T.op(T.POOL, lambda h: h.tensor_tensor(out=v3(on[:, :]), in0=v3(osum[:, :]), in1=bc_h(dnw[:]), op=ALU.mult), R=[osum.b, dnw.b], W=[on.b])
            pi = t % 4
            p, pb = PS[pi], [PB[2 * pi], PB[2 * pi + 1]]
            pbf = p[:, :].bitcast(BF16)
            for h_ in range(8):
                T.op(T.PE, lambda h: h.transpose(pbf[:, h_ * 128:(h_ + 1) * 128], on[:, h_ * 128:(h_ + 1) * 128], G.identb[:]),
                     R=[on.b, G.identb.b], W=[pb[0]], inc=(h_ == 7))
            o = oo[s]
            T.op(T.DVE, lambda h: h.tensor_tensor(out=o[:].rearrange("p h t -> p (h t)"), in0=pbf[:, 0:1024], in1=sz[s][:].rearrange("p h t -> p (h t)"), op=ALU.mult),
                 R=[pb[0], sz[s].b], W=[o.b])
            T.dma(T.ACT, G.mixT[0:1024, t * 128:(t + 1) * 128].rearrange("(h e) t -> e h t", e=128), o[:], R=[o.b], W=[G.dbuf("mixT")], sb=o.b)


def phase_e2(G, l):
    T, nc, NTOK = G.T, G.nc, G.NTOK
    PB = G.PB
    TBE = 512
    WD = min(NTOK // 2, 4096)
    NW = NTOK // WD
    SUB = WD // TBE
    with Phase(G, "e2") as ph:
        diags = [ph.sb(f"diag{i}", [128, 3, 128], BF16) for i in range(2)]
        dcur = [0]
        ain = [ph.sb(f"ain{i}", [128, WD + 2], BF16, dma=True) for i in range(2)]
        bin_ = [ph.sb(f"bin{i}", [128, WD], BF16, dma=True) for i in range(2)]
        sl = [ph.sb(f"sl{i}", [128, TBE], F32) for i in range(2)]
        ob = [ph.sb(f"ob{i}", [128, WD], BF16, dma=True) for i in range(2)]
        cnt = 0
        wc = 0
        for f in range(44):
            dcur[0] += 1
            diag = diags[dcur[0] % 2]
            make_diags(G, diag, l, C_CFFN, f)
            for wi in range(NW):
                t0 = wi * WD
                a, b, o = ain[wc % 2], bin_[wc % 2], ob[wc % 2]
                wc += 1
                T.dma(T.SP, a[:], G.upA[f * 128:(f + 1) * 128, t0:t0 + WD + 2], R=[G.dbuf("upA")], W=[a.b], sb=a.b)
                halo_fix(G, a, t0, WD, WD + 2)
                T.dma(T.SP, b[:], G.upB[f * 128:(f + 1) * 128, t0:t0 + WD], R=[G.dbuf("upB")], W=[b.b], sb=b.b)
                for j in range(SUB):
                    c0 = j * TBE
                    s_ = sl[cnt % 2]
                    bi = cnt % 4
                    cnt += 1
                    for tap in range(3):
                        T.op(T.PE, lambda h: h.matmul(bank_ap(G, bi), lhsT=diag[:, tap, :], rhs=a[:, c0 + tap:c0 + tap + TBE], start=(tap == 0), stop=(tap == 2)),
                             R=[diag.b, a.b], W=[PB[bi]], inc=(tap == 2))
                    T.op(T.ACT, lambda h: h.activation(out=s_[:], in_=bank_ap(G, bi), func=AF.Silu), R=[PB[bi]], W=[s_.b])
                    T.op(T.DVE, lambda h: h.tensor_tensor(out=o[:, c0:c0 + TBE], in0=s_[:], in1=b[:, c0:c0 + TBE], op=ALU.mult), R=[s_.b, b.b], W=[o.b])
                T.dma(T.ACT, G.gfT[f * 128:(f + 1) * 128, t0:t0 + WD], o[:], R=[o.b], W=[G.dbuf("gfT")], sb=o.b)


def phase_init(G):
    T = G.T
    NTOK = G.NTOK
    with Phase(G, "init") as ph:
        z = ph.sb("z", [128, 64], BF16, dma=True)
        T.op(T.DVE, lambda h: h.memset(z[:], 0.0), W=[z.b])
        with G.nc.allow_non_contiguous_dma(reason="one-time pad column init"):
            for (ten, nm, rows) in ((G.pT, "pT", 7168), (G.upA, "upA", D_FF)):
                n = rows // 128
                for col in (0, NTOK + 1):
                    T.dma(T.SP, ten[:, col:col + 1].rearrange("(n p) o -> p n o", p=128), z[:, 0:n].unsqueeze(2), R=[z.b], W=[G.dbuf(nm)], sb=z.b)


NTOK_CORE = 16384
_PROG = {}


def _get_prog():
    if "G" not in _PROG:
        _PROG["G"] = build_program(NTOK_CORE, DEPTH)
    return _PROG["G"]


def kernel(x_prompt, x_sample, norm_mix_pre, w_in, conv_qkv, a_log, dt_bias, dn_norm, conv_sc, sc_norm,
           w_out, norm_mix_post, norm_ffn_pre, w_up, conv_ffn, w_down, norm_ffn_post):
    f32 = np.float32
    xp = np.asarray(x_prompt, f32)
    xs = np.asarray(x_sample, f32)
    inp = dict(norm_mix_pre=norm_mix_pre, norm_mix_post=norm_mix_post, norm_ffn_pre=norm_ffn_pre, norm_ffn_post=norm_ffn_post,
               sc_norm=sc_norm, conv_qkv=conv_qkv, conv_sc=conv_sc, conv_ffn=conv_ffn, a_log=a_log, dt_bias=dt_bias, dn_norm=dn_norm)
    inp = {k: np.asarray(v, f32) for k, v in inp.items()}
    cols, gatep, dnw = prep_small(inp, DEPTH)
    G = _get_prog()
    masks = host_consts()
    w_in_, w_out_, w_up_, w_down_ = (np.ascontiguousarray(np.asarray(a, f32)) for a in (w_in, w_out, w_up, w_down))
    seqs = [xs[0], xs[1], np.concatenate([xp[0], xp[1]], axis=0)]
    flags = [1.0, 1.0, 0.0]
    zero_xT = np.zeros((D_MODEL, NTOK_CORE), f32)
    in_maps = []
    active = [0, 2, 4]
    for c in range(8):
        if c in active:
            i = active.index(c)
            xT = np.ascontiguousarray(seqs[i].T)
            fl = np.full((128, 4), flags[i], f32)
        else:
            xT = zero_xT
            fl = np.ones((128, 4), f32)
        in_maps.append(dict(xT=xT, w_in=w_in_, w_out=w_out_, w_up=w_up_, w_down=w_down_, cols=cols, gatep=gatep, dnw=dnw,
                            masks=masks, flags=fl))
    res = run_bass_kernel_spmd(G.nc, in_maps, core_ids=list(range(8)))
    ys = [np.asarray(res.results[c]["yT"], f32).T for c in active]
    y_sample = np.stack([ys[0], ys[1]], axis=0)
    y_prompt = np.stack([ys[2][:8192], ys[2][8192:]], axis=0)
    return (np.ascontiguousarray(y_prompt), np.ascontiguousarray(y_sample))
```
